# Optimizing a Trainium2 kernel written in Bass

```python
import math
import jax, jax.numpy as jnp
from jax import lax
import numpy as np

D_MODEL = 1024
BATCH = 16
SEQ = 2048
DEPTH = 1

HEAD_DIM = 64
BLOCK = 128
A_Q_HEADS = 16
A_KV_HEADS = 2
A_GROUP = A_Q_HEADS // A_KV_HEADS
A_WINDOW = 128
B_PATTERNS = ((128, 1), (512, 4), (2048, 16))
B_HEADS_PER_GROUP = 8
B_HEADS = B_HEADS_PER_GROUP * len(B_PATTERNS)
A_Q_W = A_Q_HEADS * HEAD_DIM
A_KV_W = A_KV_HEADS * HEAD_DIM
B_W = B_HEADS * HEAD_DIM
B_OUT_W = B_HEADS_PER_GROUP * HEAD_DIM
IN_W = A_Q_W + 2 * A_KV_W + 3 * B_W + 2 * D_MODEL
D_FF = -(-8 * D_MODEL // (3 * 256)) * 256
ROPE_THETA = 10000.0
LN_EPS = 1e-5
DEEPNORM_ALPHA = (2 * DEPTH) ** 0.25
DEEPNORM_BETA = (8 * DEPTH) ** -0.25
NEG_INF = -1e30

kernel_name = "hybrid_swa_sink_dilated_gated_deepnorm_adaln"


def layer_norm(x, g, b):
    xf = x.astype(jnp.float32)
    mu = jnp.mean(xf, axis=-1, keepdims=True)
    xc = xf - mu
    var = jnp.mean(xc * xc, axis=-1, keepdims=True)
    y = xc * lax.rsqrt(var + LN_EPS)
    return (y * g.astype(jnp.float32) + b.astype(jnp.float32)).astype(x.dtype)


def rope(x, positions):
    half = HEAD_DIM // 2
    inv = ROPE_THETA ** (-jnp.arange(half, dtype=jnp.float32) / half)
    ang = positions.astype(jnp.float32)[..., None] * inv
    cos = jnp.cos(ang)[:, :, None, :]
    sin = jnp.sin(ang)[:, :, None, :]
    xf = x.astype(jnp.float32)
    x1, x2 = xf[..., :half], xf[..., half:]
    out = jnp.concatenate([x1 * cos - x2 * sin, x2 * cos + x1 * sin], axis=-1)
    return out.astype(x.dtype)


def banded_window_attention(q, k, v, n_back, sink=None):
    N, T, Hkv, G, dh = q.shape
    nblk = -(-T // BLOCK)
    pad = nblk * BLOCK - T
    qp = jnp.pad(q, ((0, 0), (0, pad), (0, 0), (0, 0), (0, 0)))
    kp = jnp.pad(k, ((0, 0), (BLOCK, pad), (0, 0), (0, 0)))
    vp = jnp.pad(v, ((0, 0), (BLOCK, pad), (0, 0), (0, 0)))
    qb = qp.reshape(N, nblk, BLOCK, Hkv, G, dh)
    kb = kp.reshape(N, nblk + 1, BLOCK, Hkv, dh)
    vb = vp.reshape(N, nblk + 1, BLOCK, Hkv, dh)
    kw = jnp.concatenate([kb[:, :-1], kb[:, 1:]], axis=2)
    vw = jnp.concatenate([vb[:, :-1], vb[:, 1:]], axis=2)
    s = jnp.einsum('nbqhgd,nbkhd->nbhgqk', qb, kw,
                   preferred_element_type=jnp.float32) * (dh ** -0.5)
    qi = jnp.arange(BLOCK)[:, None]
    ki = jnp.arange(2 * BLOCK)[None, :]
    dist = qi + BLOCK - ki
    kpos = jnp.arange(nblk)[:, None, None] * BLOCK + ki[None] - BLOCK
    mask = (dist >= 0) & (dist <= n_back) & (kpos >= 0)
    s = jnp.where(mask[None, :, None, None], s, NEG_INF)
    m = jnp.max(s, axis=-1)
    if sink is not None:
        sk = sink.astype(jnp.float32).reshape(Hkv, G)[None, None, :, :, None]
        m = jnp.maximum(m, sk)
    p = jnp.exp(s - m[..., None])
    denom = jnp.sum(p, axis=-1)
    if sink is not None:
        denom = denom + jnp.exp(sk - m)
    o = jnp.einsum('nbhgqk,nbkhd->nbqhgd', p, vw.astype(jnp.float32))
    denom_t = jnp.moveaxis(denom, -1, 2)
    o = o / denom_t[..., None]
    lse = jnp.moveaxis(m, -1, 2) + jnp.log(denom_t)
    o = o.reshape(N, nblk * BLOCK, Hkv, G, dh)[:, :T]
    lse = lse.reshape(N, nblk * BLOCK, Hkv, G)[:, :T]
    return o.astype(q.dtype), lse


def dilated_attention(q, k, v, window, dilation):
    Bn, T, H, dh = q.shape
    r = dilation
    tsub = -(-T // r)
    pad = tsub * r - T

    def to_strided(t):
        t = jnp.pad(t, ((0, 0), (0, pad), (0, 0), (0, 0)))
        return t.reshape(Bn, tsub, r, H, dh).transpose(0, 2, 1, 3, 4).reshape(Bn * r, tsub, H, dh)

    o, lse = banded_window_attention(to_strided(q)[:, :, :, None], to_strided(k), to_strided(v),
                                     window // r)
    o = o[:, :, :, 0].reshape(Bn, r, tsub, H, dh).transpose(0, 2, 1, 3, 4).reshape(Bn, tsub * r, H, dh)[:, :T]
    lse = lse[:, :, :, 0].reshape(Bn, r, tsub, H).transpose(0, 2, 1, 3).reshape(Bn, tsub * r, H)[:, :T]
    return o, lse


def mixer(u, positions, w_in, sinks, w_branch_a, w_branch_b, w_o):
    Bn, T, _ = u.shape
    proj = jnp.einsum('btd,de->bte', u, w_in)
    sizes = [A_Q_W, A_KV_W, A_KV_W, B_W, B_W, B_W, D_MODEL]
    offs, acc = [], 0
    for s_ in sizes:
        acc += s_
        offs.append(acc)
    qa, ka, va, qb, kb, vb, ga, gb = jnp.split(proj, offs, axis=-1)

    qa = rope(qa.reshape(Bn, T, A_Q_HEADS, HEAD_DIM), positions).reshape(Bn, T, A_KV_HEADS, A_GROUP, HEAD_DIM)
    ka = rope(ka.reshape(Bn, T, A_KV_HEADS, HEAD_DIM), positions)
    va = va.reshape(Bn, T, A_KV_HEADS, HEAD_DIM)
    oa, _ = banded_window_attention(qa, ka, va, A_WINDOW - 1, sinks)
    ya = jnp.einsum('bte,ed->btd', oa.reshape(Bn, T, A_Q_W), w_branch_a)

    qb = rope(qb.reshape(Bn, T, B_HEADS, HEAD_DIM), positions)
    kb = rope(kb.reshape(Bn, T, B_HEADS, HEAD_DIM), positions)
    vb = vb.reshape(Bn, T, B_HEADS, HEAD_DIM)
    outs, lses = [], []
    for g, (window, dil) in enumerate(B_PATTERNS):
        sl = slice(g * B_HEADS_PER_GROUP, (g + 1) * B_HEADS_PER_GROUP)
        o_g, l_g = dilated_attention(qb[:, :, sl], kb[:, :, sl], vb[:, :, sl], window, dil)
        outs.append(o_g)
        lses.append(l_g)
    o_all = jnp.stack(outs).astype(jnp.float32)
    wts = jax.nn.softmax(jnp.stack(lses), axis=0)
    ob = jnp.sum(wts[..., None] * o_all, axis=0).astype(u.dtype)
    yb = jnp.einsum('bte,ed->btd', ob.reshape(Bn, T, B_OUT_W), w_branch_b)

    merged = jax.nn.sigmoid(ga) * ya + jax.nn.sigmoid(gb) * yb
    return jnp.einsum('btd,de->bte', merged, w_o)


def swiglu(u, w_gate_up, w_down):
    h = jnp.einsum('btd,df->btf', u, w_gate_up)
    hg, hu = jnp.split(h, 2, axis=-1)
    return jnp.einsum('btf,fd->btd', jax.nn.silu(hg) * hu, w_down)


def setup_inputs(seed: int = 0) -> dict:
    key = jax.random.key(seed)
    ks = jax.random.split(key, 20)
    f32 = jnp.float32
    nrm = lambda k, shape, scale: jax.random.normal(k, shape, f32) * scale
    x = jax.random.normal(ks[0], (BATCH, SEQ, D_MODEL), f32)
    c = jax.random.normal(ks[1], (BATCH, D_MODEL), f32)
    offset = jax.random.randint(ks[2], (BATCH, 1), 0, 1024, dtype=jnp.int32)
    positions = offset + jnp.arange(SEQ, dtype=jnp.int32)[None, :]
    return {
        "x": x,
        "c": c,
        "positions": positions,
        "w_ada": nrm(ks[3], (DEPTH, D_MODEL, 6 * D_MODEL), 0.1 * D_MODEL ** -0.5),
        "b_ada": nrm(ks[4], (DEPTH, 6 * D_MODEL), 0.01),
        "w_in": nrm(ks[5], (DEPTH, D_MODEL, IN_W), D_MODEL ** -0.5),
        "sinks": nrm(ks[6], (DEPTH, A_Q_HEADS), 1.0),
        "w_branch_a": nrm(ks[7], (DEPTH, A_Q_W, D_MODEL), A_Q_W ** -0.5),
        "w_branch_b": nrm(ks[8], (DEPTH, B_OUT_W, D_MODEL), B_OUT_W ** -0.5),
        "w_o": nrm(ks[9], (DEPTH, D_MODEL, D_MODEL), DEEPNORM_BETA * D_MODEL ** -0.5),
        "ln1_g": 1.0 + nrm(ks[10], (DEPTH, D_MODEL), 0.02),
        "ln1_b": nrm(ks[11], (DEPTH, D_MODEL), 0.02),
        "w_gate_up": nrm(ks[12], (DEPTH, D_MODEL, 2 * D_FF), D_MODEL ** -0.5),
        "w_down": nrm(ks[13], (DEPTH, D_FF, D_MODEL), DEEPNORM_BETA * D_FF ** -0.5),
        "ln2_g": 1.0 + nrm(ks[14], (DEPTH, D_MODEL), 0.02),
        "ln2_b": nrm(ks[15], (DEPTH, D_MODEL), 0.02),
    }


def reference(x, c, positions, w_ada, b_ada, w_in, sinks, w_branch_a, w_branch_b, w_o,
              ln1_g, ln1_b, w_gate_up, w_down, ln2_g, ln2_b):
    c_act = jax.nn.silu(c)
    for l in range(DEPTH):
        mod = (jnp.einsum('bd,de->be', c_act, w_ada[l]) + b_ada[l])[:, None, :]
        shift_m, scale_m, gate_m, shift_f, scale_f, gate_f = jnp.split(mod, 6, axis=-1)
        u = x * (1.0 + scale_m) + shift_m
        y = mixer(u, positions, w_in[l], sinks[l], w_branch_a[l], w_branch_b[l], w_o[l])
        x = layer_norm(DEEPNORM_ALPHA * x + (1.0 + gate_m) * y, ln1_g[l], ln1_b[l])
        u = x * (1.0 + scale_f) + shift_f
        y = swiglu(u, w_gate_up[l], w_down[l])
        x = layer_norm(DEEPNORM_ALPHA * x + (1.0 + gate_f) * y, ln2_g[l], ln2_b[l])
    return x
```

```python
import math
import os
import numpy as np
LVL = int(os.environ.get('LVL', '9'))
import concourse.bass as bass
import concourse.mybir as mybir
from concourse.bass_utils import run_bass_kernel_spmd

F32 = mybir.dt.float32
BF16 = mybir.dt.bfloat16
I32 = mybir.dt.int32
ALU = mybir.AluOpType
AF = mybir.ActivationFunctionType

NCORES = 8
SEQ = 2048
D = 1024
NSEQ = 2
DFF = 2816
ALPHA = 2.0 ** 0.25
EPS = 1e-5
QA0, KA0, VA0, QB0, KB0, VB0, GA0, GB0 = 0, 1024, 1152, 1280, 2816, 4352, 5888, 6912

COMPUTE = ("pe", "act", "dve", "pool")
ENGS = ("pe", "act", "dve", "pool", "sp")


class Res:
    __slots__ = ("name", "last_w", "readers", "sem", "dma_n", "excl")

    def __init__(self, name, excl=False):
        self.name = name
        self.excl = excl
        self.last_w = None
        self.readers = []
        self.sem = None
        self.dma_n = 0


class Op:
    __slots__ = ("eng", "fn", "dma", "deps", "sig", "seq", "semres", "idx")


class Sched:
    def __init__(self):
        self.ops = []
        self.last_on = {e: None for e in ENGS}
        self.bar = {e: set() for e in ENGS}
        self.dma_since_bar = []

    def op(self, eng, fn, reads=(), writes=(), dma=False, semres=None):
        o = Op()
        o.eng, o.fn, o.dma, o.sig, o.seq = eng, fn, dma, False, 0
        o.idx = len(self.ops)
        raw, other = set(), set()
        for r in reads:
            if r.last_w is not None:
                raw.add(r.last_w)
            if r.excl:
                other.update(r.readers)
        for r in writes:
            if r.last_w is not None:
                other.add(r.last_w)
            other.update(r.readers)
        for r in reads:
            r.readers.append(o.idx)
        for r in writes:
            r.last_w = o.idx
            r.readers = []
        deps = set()
        for d in raw | other | self.bar[eng]:
            a = self.ops[d]
            if a.dma or dma or a.eng != eng:
                deps.add(d)
            elif eng != "pe" and d in raw:
                deps.add(d)
        self.bar[eng] = set()
        o.deps = sorted(deps)
        for d in o.deps:
            self.ops[d].sig = True
        if dma:
            if semres is None:
                semres = writes[0]
            o.semres = semres
            semres.dma_n += 1
            o.seq = semres.dma_n
            self.dma_since_bar.append(o.idx)
        else:
            o.semres = None
        self.ops.append(o)
        self.last_on[eng] = o.idx
        return o

    def barrier(self):
        s = set(self.dma_since_bar)
        for e in ENGS:
            if self.last_on[e] is not None:
                s.add(self.last_on[e])
        for e in ENGS:
            self.bar[e] = set(s) | self.bar[e]
        self.dma_since_bar = []

    def emit(self, nc, final_res=()):
        from contextlib import ExitStack
        cnt = {e: 0 for e in COMPUTE}
        for o in self.ops:
            if not o.dma and o.sig:
                cnt[o.eng] += 1
                o.seq = cnt[o.eng]
        with ExitStack() as es:
            esem = {e: es.enter_context(nc.semaphore("s_" + e)) for e in COMPUTE}
            nsem = 0
            for o in self.ops:
                if o.dma and o.semres.sem is None:
                    o.semres.sem = es.enter_context(nc.semaphore("d%d" % nsem))
                    nsem += 1
            self.nsem = nsem + 4
            block = es.enter_context(nc.Block())
            ops = self.ops

            def run(eng_name, eng):
                waited = {}
                for o in ops:
                    if o.eng != eng_name:
                        continue
                    need = {}
                    for d in o.deps:
                        a = ops[d]
                        if a.dma:
                            sem, val = a.semres.sem, 16 * a.seq
                        else:
                            sem, val = esem[a.eng], a.seq
                        k = id(sem)
                        if waited.get(k, 0) < val and need.get(k, (None, 0))[1] < val:
                            need[k] = (sem, val)
                    for k, (sem, val) in need.items():
                        eng.wait_ge(sem, val)
                        waited[k] = val
                    ins = o.fn(eng)
                    if o.dma:
                        ins.then_inc(o.semres.sem, 16)
                    elif o.sig:
                        ins.then_inc(esem[o.eng], 1)
                if eng_name == "sp":
                    for r in final_res:
                        if r.sem is not None and r.dma_n > 0:
                            eng.wait_ge(r.sem, 16 * r.dma_n)

            @block.tensor
            def _(e):
                run("pe", e)

            @block.scalar
            def _(e):
                run("act", e)

            @block.vector
            def _(e):
                run("dve", e)

            @block.gpsimd
            def _(e):
                run("pool", e)

            @block.sync
            def _(e):
                run("sp", e)


class Ring:
    def __init__(self, items):
        self.items = items
        self.i = 0

    def next(self):
        r = self.items[self.i % len(self.items)]
        self.i += 1
        return r


DT_BYTES = {F32: 4, BF16: 2, I32: 4}
KB = 1024


class _Stop(Exception):
    pass


def build_program(taps=(), stop=None):
    nc = bass.Bass("TRN2", target_bir_lowering=False)

    def ck(name):
        if stop == name:
            raise _Stop()
    S = Sched()
    uid = [0]

    def din(name, shape, dt=F32):
        return nc.dram_tensor(name, shape, dt, kind="ExternalInput").ap()

    x = din("x", [NSEQ * SEQ, D])
    cT = din("cT", [128, 8, NSEQ])
    pos = din("pos", [NSEQ, SEQ], I32)
    w_ada = din("w_ada", [D, 6 * D])
    b_ada = din("b_ada", [1, 6 * D])
    w_in = din("w_in", [D, 7936])
    sinks = din("sinks", [1, 16])
    w_a = din("w_a", [1024, D])
    w_b = din("w_b", [512, D])
    w_o = din("w_o", [D, D])
    ln_d = [din(n, [1, D]) for n in ("ln1_g", "ln1_b", "ln2_g", "ln2_b")]
    w_gu = din("w_gu", [D, 2 * DFF])
    w_dn = din("w_dn", [DFF, D])
    c_ident = din("c_ident", [128, 128])
    c_rm = din("c_rm", [128, 128])
    c_mska = din("c_mska", [128, 512])
    c_mskb = din("c_mskb", [128, 512])
    c_vecs = din("c_vecs", [128, 2])
    out = nc.dram_tensor("out", [NSEQ * SEQ, D], F32, kind="ExternalOutput").ap()
    tap_out = {}
    for (tn, shp) in taps:
        tap_out[tn] = nc.dram_tensor("tap_" + tn, list(shp), F32, kind="ExternalOutput").ap()

    SB_LO = 17408
    PERS = 28 * KB
    AR0 = SB_LO + PERS
    SB_TOP = 221184
    AR_SZ = SB_TOP - AR0

    class Arena:
        def __init__(self, lo, hi):
            self.lo, self.hi, self.p = lo, hi, lo

        def alloc(self, name, shape, dt):
            n = DT_BYTES[dt]
            for s_ in shape[1:]:
                n *= s_
            off = (self.p + 63) // 64 * 64
            self.p = off + n
            assert self.p <= self.hi, (name, self.p, self.hi)
            uid[0] += 1
            return nc.alloc_sbuf_tensor_at("%s_%d" % (name, uid[0]), list(shape), dt, offset=off)

    pers = Arena(SB_LO, SB_LO + PERS)
    ident = pers.alloc("ident", [128, 128], F32)
    rm = pers.alloc("rm", [128, 128], BF16)
    mska = pers.alloc("mska", [128, 2, 256], BF16)
    mskb = pers.alloc("mskb", [128, 2, 256], BF16)
    vecs = pers.alloc("vecs", [128, 2], F32)
    exps = pers.alloc("exps", [128, 16], F32)
    modT = pers.alloc("modT", [128, 4, 8, NSEQ], F32)
    cact = pers.alloc("cact", [128, 8, NSEQ], F32)
    ones_r = pers.alloc("ones_r", [1, 128], F32)
    sinkE = pers.alloc("sinkE", [128, 8], F32)
    sinkO = pers.alloc("sinkO", [128, 8], F32)
    bc = [pers.alloc("bc%d" % i, [128, D], F32) for i in range(6)]
    r_const = Res("const")
    r_modT = Res("modT")
    r_crep = Res("crep")
    r_bc = [Res("bc%d" % i) for i in range(6)]

    banks = []
    pairs = []
    for i in range(4):
        t = nc.alloc_psum_tensor("bpair%d" % i, [128, 1024], F32)
        rs = [Res("bank%d" % (2 * i + j), excl=True) for j in range(2)]
        pairs.append((t, rs))
        for j in range(2):
            banks.append((t[:, j * 512:(j + 1) * 512], rs[j]))

    def dma(q, out_ap, in_ap, reads=(), writes=(), semres=None):
        return S.op(q, lambda e: e.dma_start(out=out_ap, in_=in_ap), reads=reads, writes=writes,
                    dma=True, semres=semres)

    def mm(out_ap, lhsT, rhs, start, stop, reads, writes):
        return S.op("pe", lambda e: e.matmul(out_ap, lhsT=lhsT, rhs=rhs, start=start, stop=stop),
                    reads=reads, writes=writes)

    tap_res = []

    def tap(name, ap, reads):
        if name in tap_out:
            r = Res("tap_" + name)
            tap_res.append(r)
            dma("pool", tap_out[name], ap, reads=reads, semres=r)

    dma("sp", ident[:], c_ident, writes=[r_const])
    dma("sp", vecs[:], c_vecs, writes=[r_const])
    dma("pool", rm[:], c_rm, writes=[r_const])
    dma("pool", mska[:], c_mska.rearrange("p (h q) -> p h q", h=2), writes=[r_const])
    dma("pool", mskb[:], c_mskb.rearrange("p (h q) -> p h q", h=2), writes=[r_const])
    dma("sp", exps[:], sinks[0:1, :].partition_broadcast(128), writes=[r_const])
    dma("sp", cact[:], cT, writes=[r_const])
    for i in range(4):
        dma("sp", bc[i][:], ln_d[i][0:1, :].partition_broadcast(128), writes=[r_bc[i]])
    S.op("act", lambda e: e.activation(out=exps[:], in_=exps[:], func=AF.Exp), reads=[r_const], writes=[r_const])
    S.op("act", lambda e: e.activation(out=cact[:], in_=cact[:], func=AF.Silu), reads=[r_const], writes=[r_const])
    S.op("pool", lambda e: e.memset(ones_r[:], 1.0), writes=[r_const])
    S.op("dve", lambda e: e.memset(sinkE[:], 0.0), reads=[r_const], writes=[r_const])
    S.op("dve", lambda e: e.memset(sinkO[:], 0.0), reads=[r_const], writes=[r_const])
    S.op("dve", lambda e: e.tensor_copy(out=sinkE[64:128, :], in_=exps[:].rearrange("p (j t) -> p j t", t=2)[64:128, :, 0]),
         reads=[r_const], writes=[r_const])
    S.op("dve", lambda e: e.tensor_copy(out=sinkO[0:64, :], in_=exps[:].rearrange("p (j t) -> p j t", t=2)[0:64, :, 1]),
         reads=[r_const], writes=[r_const])
    S.barrier()
    gu_s = nc.dram_tensor("gu_s", [22, 128, 8 * 2 * 128], BF16, kind="Internal").ap()
    r_gus = Res("gus")
    for j in range(22):
        for g in range(2):
            dma("pool", gu_s[j].rearrange("p (k g c) -> p k g c", k=8, g=2)[:, :, g, :],
                w_gu[:, g * DFF + j * 128:g * DFF + (j + 1) * 128].rearrange("(k p) c -> p k c", p=128), writes=[r_gus])

    final_res = []

    def mod_columns():
        ar = Arena(AR0, AR0 + AR_SZ)
        wsl = [(ar.alloc("wada", [128, 8, 512], F32), Res("wada%d" % i)) for i in range(2)]
        bsl = [(ar.alloc("brow", [1, 512], F32), Res("brow%d" % i)) for i in range(2)]
        wr, br = Ring(wsl), Ring(bsl)
        pr = Ring(banks[0:2])
        for j, kind in enumerate((0, 1, 3, 4)):
            for half in range(2):
                c0 = kind * D + half * 512
                (wt, wres), (bt, bres) = wr.next(), br.next()
                dma("sp", wt[:], w_ada[:, c0:c0 + 512].rearrange("(k p) c -> p k c", p=128), writes=[wres])
                dma("sp", bt[:], b_ada[0:1, c0:c0 + 512], writes=[bres])
                for e4 in range(4):
                    (pt, pres) = pr.next()
                    for k in range(8):
                        mm(pt[:, 0:NSEQ], wt[:, k, e4 * 128:(e4 + 1) * 128], cact[:, k, :], k == 0, False,
                           [wres, r_const], [pres])
                    mm(pt[:, 0:NSEQ], bt[0:1, e4 * 128:(e4 + 1) * 128], ones_r[0:1, 0:NSEQ], False, True,
                       [bres, r_const], [pres])
                    kk = half * 4 + e4
                    addv = 1.0 if kind in (1, 4) else 0.0
                    S.op("dve", lambda e, pt=pt, j=j, kk=kk, addv=addv: e.tensor_scalar(
                        out=modT[:, j, kk, :], in0=pt[:, 0:NSEQ], scalar1=addv, scalar2=None, op0=ALU.add),
                        reads=[pres], writes=[r_modT])

    def mod_gate_rows(s):
        ar = Arena(AR0, AR0 + AR_SZ)
        wsl = [(ar.alloc("wadg", [128, 8, 512], F32), Res("wadg%d" % i)) for i in range(2)]
        bsl = [(ar.alloc("browg", [1, 512], F32), Res("browg%d" % i)) for i in range(2)]
        crep = ar.alloc("crep", [128, 8, 128], F32)
        wr, br = Ring(wsl), Ring(bsl)
        pr = Ring(banks[0:2])
        for k in range(8):
            S.op("dve", lambda e, k=k: e.tensor_scalar(out=crep[:, k, :], in0=ident[:], scalar1=0.0,
                                                       scalar2=cact[:, k, s:s + 1], op0=ALU.mult, op1=ALU.add),
                 reads=[r_const], writes=[r_crep])
        for gi, kind in enumerate((2, 5)):
            for half in range(2):
                c0 = kind * D + half * 512
                (wt, wres), (bt, bres) = wr.next(), br.next()
                dma("sp", wt[:], w_ada[:, c0:c0 + 512].rearrange("(k p) c -> p k c", p=128), writes=[wres])
                dma("sp", bt[:], b_ada[0:1, c0:c0 + 512], writes=[bres])
                (pt, pres) = pr.next()
                for k in range(8):
                    mm(pt[:], crep[:, k, :], wt[:, k, :], k == 0, False, [wres, r_crep], [pres])
                mm(pt[:], ones_r[0:1, :], bt[0:1, :], False, True, [bres, r_const], [pres])
                S.op("dve", lambda e, pt=pt, gi=gi, half=half: e.tensor_scalar(
                    out=bc[4 + gi][:, half * 512:(half + 1) * 512], in0=pt[:], scalar1=1.0, scalar2=None, op0=ALU.add),
                    reads=[pres], writes=[r_bc[4 + gi]])

    try:
        ck('const')
        mod_columns()
        S.barrier()
        ck('mod')

        for s in range(int(os.environ.get('NS', NSEQ))):
            mod_gate_rows(s)
            S.barrier()
            ck('gate')
            fx = Arena(AR0, AR0 + 96 * KB)
            uT = fx.alloc("uT", [128, 8, SEQ], BF16)
            cosT = fx.alloc("cosT", [128, SEQ], F32)
            sinS = fx.alloc("sinS", [128, SEQ], F32)
            oa_off = (fx.p + 63) // 64 * 64
            OA = fx.alloc("OA", [128, 8, SEQ], BF16)
            OB = fx.alloc("OB", [128, 4, SEQ], BF16)
            r_uT = [Res("uT%d" % t) for t in range(4)]
            r_cs = Res("cs")
            r_OA = [Res("OA%d" % j) for j in range(8)]
            r_OB = [Res("OB%d" % j) for j in range(4)]
            ub = Arena(AR0 + 96 * KB, AR0 + AR_SZ)
            wsl = Ring([(ub.alloc("wu", [128, 8, 3, 128], BF16), Res("wu%d" % i)) for i in range(2)])
            QTr = Ring([(ub.alloc("QT", [128, SEQ], BF16), Res("QT%d" % i)) for i in range(2)])
            KTr_items = [(ub.alloc("KT", [128, SEQ], BF16), Res("KT%d" % i)) for i in range(2)]
            KTr = Ring(KTr_items)
            VOr_items = [(ub.alloc("VO", [128, 16, 2, 128], BF16), Res("VO%d" % i)) for i in range(2)]
            VOr = Ring(VOr_items)
            acc_mark = ub.p
            accO = ub.alloc("accO", [128, SEQ], F32)
            accD = ub.alloc("accD", [128, SEQ], F32)
            r_accs = [Res("acc%d" % k) for k in range(4)]
            Pr = [(ub.alloc("P", [128, 2, 256], BF16), Res("P%d" % i)) for i in range(4)]
            e_off = (ub.p + 63) // 64 * 64
            Er = Ring([(ub.alloc("E", [128, 2, 256], BF16), Res("E%d" % i)) for i in range(2)])
            xbr = Ring([(ub.alloc("xb", [128, 512], BF16), Res("xb%d" % i)) for i in range(2)])
            t1r = Ring([(ub.alloc("t1", [128, 512], F32), Res("t1_%d" % i)) for i in range(2)])
            t2b = nc.alloc_sbuf_tensor_at("t2b_%d" % s, [128, 512], F32, offset=e_off)
            t2r = Ring([(ub.alloc("t2", [128, 512], F32), [Res("t2_0")]), (t2b, [Er.items[0][1], Er.items[1][1]])])
            tr = Arena(acc_mark, acc_mark + 16 * KB)
            posi = tr.alloc("posi", [128, SEQ], I32)
            xsr = Ring([(tr.alloc("xs", [128, D], F32), Res("xs%d" % i)) for i in range(2)])
            r_posi = Res("posi")
            dma("sp", posi[:], pos[s:s + 1, :].partition_broadcast(128), writes=[r_posi])
            tmpa = nc.alloc_sbuf_tensor_at("tmpa_%d" % s, [128, SEQ], F32, offset=oa_off)
            I2P = 1.0 / (2 * math.pi)
            rr = [r_posi, r_cs]
            S.op("dve", lambda e: e.tensor_copy(out=cosT[:], in_=posi[:]), reads=rr, writes=rr)
            S.op("dve", lambda e: e.tensor_scalar(out=cosT[:], in0=cosT[:], scalar1=vecs[:, 0:1], scalar2=None, op0=ALU.mult),
                 reads=rr + [r_const], writes=rr)
            S.op("dve", lambda e: e.tensor_scalar(out=sinS[:], in0=cosT[:], scalar1=I2P, scalar2=None, op0=ALU.mult), reads=rr, writes=rr)
            S.op("dve", lambda e: e.tensor_copy(out=posi[:], in_=sinS[:]), reads=rr, writes=rr)
            S.op("dve", lambda e: e.tensor_copy(out=sinS[:], in_=posi[:]), reads=rr, writes=rr)
            S.op("dve", lambda e: e.scalar_tensor_tensor(out=sinS[:], in0=sinS[:], scalar=-2 * math.pi, in1=cosT[:],
                                                         op0=ALU.mult, op1=ALU.add), reads=rr, writes=rr)
            S.op("dve", lambda e: e.tensor_scalar(out=sinS[:], in0=sinS[:], scalar1=math.pi, scalar2=-math.pi,
                                                  op0=ALU.min, op1=ALU.max), reads=rr, writes=rr)
            S.op("dve", lambda e: e.tensor_scalar(out=tmpa[:], in0=cosT[:], scalar1=0.5 * math.pi, scalar2=None, op0=ALU.add),
                 reads=rr, writes=rr)
            S.op("dve", lambda e: e.tensor_scalar(out=cosT[:], in0=tmpa[:], scalar1=I2P, scalar2=None, op0=ALU.mult), reads=rr, writes=rr)
            S.op("dve", lambda e: e.tensor_copy(out=posi[:], in_=cosT[:]), reads=rr, writes=rr)
            S.op("dve", lambda e: e.tensor_copy(out=cosT[:], in_=posi[:]), reads=rr, writes=rr)
            S.op("dve", lambda e: e.scalar_tensor_tensor(out=cosT[:], in0=cosT[:], scalar=-2 * math.pi, in1=tmpa[:],
                                                         op0=ALU.mult, op1=ALU.add), reads=rr, writes=rr)
            S.op("dve", lambda e: e.tensor_scalar(out=cosT[:], in0=cosT[:], scalar1=math.pi, scalar2=-math.pi,
                                                  op0=ALU.min, op1=ALU.max), reads=rr, writes=rr)
            S.op("act", lambda e: e.activation(out=sinS[:], in_=sinS[:], func=AF.Sin), reads=rr, writes=rr)
            S.op("act", lambda e: e.activation(out=cosT[:], in_=cosT[:], func=AF.Sin), reads=rr, writes=rr)
            S.op("dve", lambda e: e.tensor_scalar(out=sinS[:], in0=sinS[:], scalar1=vecs[:, 1:2], scalar2=None, op0=ALU.mult),
                 reads=rr + [r_const], writes=rr)
            for i in range(16):
                (xt, xres) = xsr.next()
                dma("sp", xt[:], x[s * SEQ + i * 128: s * SEQ + (i + 1) * 128, :], writes=[xres])
                for hf in range(2):
                    (pt, pres) = banks[hf + 2 * (i % 2)]
                    for k4 in range(4):
                        k = hf * 4 + k4
                        S.op("pe", lambda e, pt=pt, k4=k4, xt=xt, k=k: e.transpose(
                            pt[:, k4 * 128:(k4 + 1) * 128], xt[:, k * 128:(k + 1) * 128], ident[:]),
                            reads=[xres, r_const], writes=[pres])
                    for k4 in range(4):
                        k = hf * 4 + k4
                        S.op("act", lambda e, pt=pt, k4=k4, k=k, i=i, s=s: e.activation(
                            out=uT[:, k, i * 128:(i + 1) * 128], in_=pt[:, k4 * 128:(k4 + 1) * 128], func=AF.Identity,
                            scale=modT[:, 1, k, s:s + 1], bias=modT[:, 0, k, s:s + 1]),
                            reads=[pres, r_modT], writes=[r_uT[i // 4]])
            S.barrier()
            tap("uT%d" % s, uT[:, 0, :], r_uT)
            tap("cos%d" % s, cosT[:], [r_cs])
            tap("sin%d" % s, sinS[:], [r_cs])
            ck('ut')

            proj_banks = Ring(banks[0:2])
            rot_banks = Ring(banks[2:4])
            s_pairs = Ring(pairs[2:4])
            pv_banks = Ring(banks[2:4])

            def load_w(cols):
                (wt, wres) = wsl.next()
                for pi, c0 in enumerate(cols):
                    dma("pool", wt[:, :, pi, :], w_in[:, c0:c0 + 128].rearrange("(k p) c -> p k c", p=128), writes=[wres])
                return wt, wres

            def proj_rope(wt, wres, part, dest, dres, r, between=None):
                def stage_b(st):
                    (tg, xb, xbres, t1, t1res) = st
                    (t2, t2res) = t2r.next()
                    (rt, rres) = rot_banks.next()
                    mm(rt[:], rm[:], xb[:], True, True, [xbres, r_const], [rres])
                    S.op("dve", lambda e, t2=t2, rt=rt, tg=tg: e.tensor_tensor(
                        out=t2[:], in0=rt[:], in1=sinS[:, tg * 512:(tg + 1) * 512], op=ALU.mult),
                        reads=[rres, r_cs], writes=t2res)
                    n = 512 // r
                    dv = dest[:].rearrange("p (c i) -> p c i", c=r)[:, :, tg * n:(tg + 1) * n]
                    a1 = t1[:].rearrange("p (i c) -> p c i", c=r)
                    a2 = t2[:].rearrange("p (i c) -> p c i", c=r)
                    S.op("pool", lambda e, dv=dv, a1=a1, a2=a2: e.tensor_tensor(out=dv, in0=a1, in1=a2, op=ALU.add),
                         reads=[t1res] + t2res, writes=[dres])

                pend = None
                for tg in range(4):
                    (pt, pres) = proj_banks.next()
                    for k in range(8):
                        mm(pt[:], wt[:, k, part, :], uT[:, k, tg * 512:(tg + 1) * 512], k == 0, k == 7,
                           [wres, r_uT[tg]], [pres])
                    (xb, xbres) = xbr.next()
                    (t1, t1res) = t1r.next()
                    S.op("act", lambda e, xb=xb, pt=pt: e.activation(out=xb[:], in_=pt[:], func=AF.Copy),
                         reads=[pres], writes=[xbres])
                    S.op("dve", lambda e, t1=t1, pt=pt, tg=tg: e.tensor_tensor(
                        out=t1[:], in0=pt[:], in1=cosT[:, tg * 512:(tg + 1) * 512], op=ALU.mult),
                        reads=[pres, r_cs], writes=[t1res])
                    if pend is not None:
                        stage_b(pend)
                    pend = (tg, xb, xbres, t1, t1res)
                    if between:
                        between.pop(0)()
                stage_b(pend)
                while between:
                    between.pop(0)()

            def proj_v(wt, wres, part, vo, vores, r, ncols_layout):
                nblk = 16 // r
                for c in range(r):
                    for kb in range(nblk):
                        (pt, pres) = proj_banks.next()
                        for k in range(8):
                            lt = uT[:, k, :].rearrange("p (i c) -> p c i", c=r)[:, c, kb * 128:(kb + 1) * 128]
                            mm(pt[:, 0:128], lt, wt[:, k, part, :], k == 0, k == 7, [wres] + r_uT, [pres])
                        for (vt, vr, cmap) in ncols_layout:
                            for (h, d0, s0) in cmap:
                                S.op("act", lambda e, vt=vt, pt=pt, h=h, d0=d0, s0=s0, c=c, kb=kb: e.activation(
                                    out=vt[:, c * nblk + kb, h, d0:d0 + 64], in_=pt[:, s0:s0 + 64], func=AF.Copy),
                                    reads=[pres], writes=[vr])

            def attention(qt, qres, kt, kres, vo, vores, r, msk, first, tog, sinkj=None, fin_cb=None):
                nblk = 16 // r
                clen = SEQ // r
                nT = (nblk + 1) // 2
                steps = [(c, m) for c in range(r) for m in range(nT)]

                def emit_S(c, m):
                    lst = []
                    for kb in range(2 * m, min(2 * m + 2, nblk)):
                        nq = 256 if kb + 1 < nblk else 128
                        (st, sresl) = s_pairs.next()
                        sv = st[:].rearrange("p (h q) -> p h q", h=2)
                        for h in range(2):
                            mm(sv[:, h, 0:nq], kt[64 * h:64 * h + 64, c * clen + kb * 128: c * clen + (kb + 1) * 128],
                               qt[64 * h:64 * h + 64, c * clen + kb * 128: c * clen + kb * 128 + nq], True, True,
                               [kres, qres], [sresl[h]])
                        lst.append((kb, nq, sv, sresl))
                    return lst

                def emit_exp(c, item):
                    (kb, nq, sv, sresl) = item
                    (et, eres) = Er.next()
                    S.op("act", lambda e, et=et, sv=sv, nq=nq: e.activation(
                        out=et[:, :, 0:nq], in_=sv[:, :, 0:nq], func=AF.Exp, scale=0.125),
                        reads=sresl, writes=[eres])
                    return (et, eres)

                def emit_mask(c, item, e_, eng):
                    (kb, nq, sv, sresl) = item
                    (et, eres) = e_
                    (pt_, pres_) = Pr[(c * nblk + kb) % 4]
                    S.op(eng, lambda e, pt_=pt_, et=et, nq=nq: e.tensor_tensor(
                        out=pt_[:, :, 0:nq], in0=et[:, :, 0:nq], in1=msk[:, :, 0:nq], op=ALU.mult),
                        reads=[eres, r_const], writes=[pres_])

                def emit_PV(c, m):
                    W = min(256, nblk * 128)
                    (pv, pvres) = pv_banks.next()
                    pvv = pv[:].rearrange("p (h q) -> p h q", h=2)
                    g0 = c * nblk
                    for h in range(2):
                        has2 = (2 * m - 1 >= 0)
                        has3 = (2 * m + 1 < nblk)
                        P0 = Pr[(g0 + 2 * m) % 4]
                        mm(pvv[:, h, 0:W], vo[:, g0 + 2 * m, h, :], P0[0][:, h, 0:W], True,
                           not (has2 or has3), [vores, P0[1]], [pvres])
                        if has2:
                            P2 = Pr[(g0 + 2 * m - 1) % 4]
                            mm(pvv[:, h, 0:128], vo[:, g0 + 2 * m - 1, h, :], P2[0][:, h, 128:256],
                               False, not has3, [vores, P2[1]], [pvres])
                        if has3:
                            P3 = Pr[(g0 + 2 * m + 1) % 4]
                            mm(pvv[:, h, 128:256], vo[:, g0 + 2 * m + 1, h, :], P3[0][:, h, 0:128],
                               False, True, [vores, P3[1]], [pvres])

                    def nat(t_, lo, hi):
                        return t_[lo:hi, :].rearrange("p (i c) -> p c i", c=r)[:, c, 256 * m:256 * m + W]
                    prs = [(nat(accO, 0, 128), pvv[:, 0, 0:W]), (nat(accD, 0, 128), pvv[:, 1, 0:W])]

                    lo = r * 256 * m + c
                    hi = lo + r * (W - 1)
                    rch = [r_accs[k] for k in range(lo // 512, hi // 512 + 1)]

                    def do_acc():
                        for hi_, (dst, src) in enumerate(prs):
                            if first and sinkj is not None:
                                sc = (sinkE if hi_ == 0 else sinkO)[:, sinkj:sinkj + 1]
                                S.op("dve", lambda e, dst=dst, src=src, sc=sc: e.tensor_scalar(
                                    out=dst, in0=src, scalar1=sc, scalar2=None, op0=ALU.add),
                                    reads=[pvres, r_const], writes=rch)
                            elif first:
                                S.op("dve", lambda e, dst=dst, src=src: e.tensor_copy(out=dst, in_=src),
                                     reads=[pvres], writes=rch)
                            else:
                                S.op("dve", lambda e, dst=dst, src=src: e.tensor_tensor(out=dst, in0=src, in1=dst, op=ALU.add),
                                     reads=[pvres] + rch, writes=rch)
                    return do_acc

                pend = None
                for t in range(len(steps) + 1):
                    cur = None
                    late = None
                    if t < len(steps):
                        c, m = steps[t]
                        lst = emit_S(c, m)
                        cur = (c, m)
                        e0 = emit_exp(c, lst[0])
                        emit_mask(c, lst[0], e0, "pool")
                        if len(lst) > 1:
                            e1 = emit_exp(c, lst[1])
                            late = (c, lst[1], e1)
                    acc_fn = None
                    if pend is not None:
                        acc_fn = emit_PV(pend[0], pend[1])
                    if late is not None:
                        emit_mask(late[0], late[1], late[2], "dve")
                    if acc_fn is not None:
                        acc_fn()
                        if fin_cb is not None and pend[1] % 2 == 1:
                            fin_cb(pend[1] // 2)
                    pend = cur

            fin_ring = Ring([(t_, [r_]) for (t_, r_) in t1r.items] + list(t2r.items))

            def finalize_chunk(dest, dres, ch, mul_eng, ring=None):
                cs_ = slice(ch * 512, (ch + 1) * 512)
                ra = [r_accs[ch]]
                (tmp, tres) = (ring or fin_ring).next()
                S.op("dve", lambda e, tmp=tmp, cs_=cs_: e.tensor_copy(out=tmp[0:64, :], in_=accO[64:128, cs_]),
                     reads=ra, writes=tres)
                S.op("dve", lambda e, tmp=tmp, cs_=cs_: e.tensor_copy(out=tmp[64:128, :], in_=accD[0:64, cs_]),
                     reads=ra, writes=tres)
                S.op("dve", lambda e, tmp=tmp: e.reciprocal(out=tmp[:], in_=tmp[:]), reads=tres, writes=tres)
                S.op(mul_eng, lambda e, tmp=tmp, cs_=cs_: e.tensor_tensor(
                    out=dest[0:64, cs_], in0=accO[0:64, cs_], in1=tmp[0:64, :], op=ALU.mult),
                    reads=ra + tres, writes=[dres])
                S.op(mul_eng, lambda e, tmp=tmp, cs_=cs_: e.tensor_tensor(
                    out=dest[64:128, cs_], in0=accD[64:128, cs_], in1=tmp[64:128, :], op=ALU.mult),
                    reads=ra + tres, writes=[dres])

            def finalize(dest, dres, sink_heads, mul_eng):
                for ch in range(4):
                    finalize_chunk(dest, dres, ch, mul_eng)

            tog = [0]
            for (vt, vr) in VOr_items:
                S.op("pool", lambda e, vt=vt: e.memset(vt[:], 1.0), writes=[vr])
            b_units = [(p, g, r) for p in range(4) for g, r in enumerate((1, 4, 16))]

            def bcols(p, g):
                hc = (g * 8 + 2 * p) * 64
                return [QB0 + hc, KB0 + hc, VB0 + hc]
            w_next = load_w(bcols(0, 0))
            pend_fin = [None]

            def flush_fin():
                if pend_fin[0] is not None:
                    finalize(*pend_fin[0])
                    pend_fin[0] = None
            for ui, (p, g, r) in enumerate(b_units):
                wt, wres = w_next
                (qt, qres) = QTr.next()
                (kt, kres) = KTr.next()
                (vo, vores) = VOr.next()
                proj_rope(wt, wres, 0, qt, qres, r)
                proj_rope(wt, wres, 1, kt, kres, r)
                ck('b1')
                proj_v(wt, wres, 2, vo, vores, r, [(vo, vores, [(0, 0, 0), (1, 64, 64)])])
                ck('b1v')
                if ui + 1 < len(b_units):
                    w_next = load_w(bcols(b_units[ui + 1][0], b_units[ui + 1][1]))
                else:
                    w_next = load_w([KA0, VA0])
                flush_fin()
                attention(qt, qres, kt, kres, vo, vores, r, mskb, g == 0, tog)
                ck('b1a')
                if g == 2:
                    pend_fin[0] = (OB[:, p, :], r_OB[p], None, "pool")
                    ck('bfin')
            tap("OB%d" % s, OB[:, 0, :], r_OB)
            ck('B')
            KAd = KTr_items
            VOA = VOr_items
            wt, wres = w_next
            w_next = load_w([QA0])
            (KAn, r_KAn) = QTr.next()
            proj_rope(wt, wres, 0, KAn, r_KAn, 1)
            for kv in range(2):
                (kd, kdres) = KAd[kv]
                for h in range(2):
                    S.op("dve", lambda e, kd=kd, kv=kv, h=h: e.tensor_copy(
                        out=kd[64 * h:64 * h + 64, :], in_=KAn[64 * kv:64 * kv + 64, :]),
                        reads=[r_KAn], writes=[kdres])
            proj_v(wt, wres, 1, None, None, 1,
                   [(VOA[0][0], VOA[0][1], [(0, 0, 0), (1, 64, 0)]), (VOA[1][0], VOA[1][1], [(0, 0, 64), (1, 64, 64)])])
            att_ring = Ring([(t_, [r_]) for (t_, r_) in t1r.items] + [t2r.items[0]])
            for j in range(8):
                wt, wres = w_next
                (qt, qres) = QTr.next()
                proj_rope(wt, wres, 0, qt, qres, 1)
                if j + 1 < 8:
                    w_next = load_w([QA0 + (j + 1) * 128])
                kv = j // 4
                flush_fin()
                attention(qt, qres, KAd[kv][0], KAd[kv][1], VOA[kv][0], VOA[kv][1], 1, mska, True, tog, sinkj=j,
                          fin_cb=(lambda ch, j=j: finalize_chunk(OA[:, j, :], r_OA[j], ch, "pool", att_ring)))
            S.barrier()
            tap("OA%d" % s, OA[:, 0, :], r_OA)
            ck('A')

            mg = nc.alloc_sbuf_tensor_at("merged_%d" % s, [128, 8, SEQ], BF16, offset=AR0 + 96 * KB)
            r_mg = [Res("mg%d" % t) for t in range(4)]
            wo_t = nc.alloc_sbuf_tensor_at("wo_%d" % s, [128, 8, D], BF16, offset=AR0 + 128 * KB)
            r_wo = Res("wo")
            pm = Arena(AR0 + 144 * KB, AR0 + AR_SZ)
            wmr = Ring([(pm.alloc("wm", [128, 28, 128], BF16), Res("wm%d" % i)) for i in range(2)])
            sar = Ring([(pm.alloc("sa", [128, 512], F32), Res("sa%d" % i)) for i in range(2)])
            sbr = Ring([(pm.alloc("sb", [128, 512], F32), Res("sb%d" % i)) for i in range(2)])
            for hf in range(2):
                dma("pool", wo_t[:, :, hf * 512:(hf + 1) * 512],
                    w_o[:, hf * 512:(hf + 1) * 512].rearrange("(k p) c -> p k c", p=128), writes=[r_wo])
            bring = Ring([banks[0:4], banks[4:8]])
            def load_wm(dch):
                (wm, wmres) = wmr.next()
                cs_ = slice(dch * 128, (dch + 1) * 128)
                dma("pool", wm[:, 0:8, :], w_a[:, cs_].rearrange("(k p) c -> p k c", p=128), writes=[wmres])
                dma("pool", wm[:, 8:12, :], w_b[:, cs_].rearrange("(k p) c -> p k c", p=128), writes=[wmres])
                dma("pool", wm[:, 12:20, :], w_in[:, GA0 + dch * 128:GA0 + (dch + 1) * 128].rearrange("(k p) c -> p k c", p=128),
                    writes=[wmres])
                dma("pool", wm[:, 20:28, :], w_in[:, GB0 + dch * 128:GB0 + (dch + 1) * 128].rearrange("(k p) c -> p k c", p=128),
                    writes=[wmres])
                return wm, wmres
            wm_next = load_wm(0)
            for dch in range(8):
                (wm, wmres) = wm_next
                if dch + 1 < 8:
                    wm_next = load_wm(dch + 1)
                for tg in range(4):
                    bk = bring.next()
                    ts_ = slice(tg * 512, (tg + 1) * 512)
                    (pya, rya), (pyb, ryb), (pga, rga), (pgb, rgb) = bk
                    for k in range(8):
                        mm(pya[:], wm[:, k, :], OA[:, k, ts_], k == 0, k == 7, [wmres, r_OA[k]], [rya])
                    for k in range(4):
                        mm(pyb[:], wm[:, 8 + k, :], OB[:, k, ts_], k == 0, k == 3, [wmres, r_OB[k]], [ryb])
                    for k in range(8):
                        mm(pga[:], wm[:, 12 + k, :], uT[:, k, ts_], k == 0, k == 7, [wmres, r_uT[tg]], [rga])
                    for k in range(8):
                        mm(pgb[:], wm[:, 20 + k, :], uT[:, k, ts_], k == 0, k == 7, [wmres, r_uT[tg]], [rgb])
                    (sa, sares) = sar.next()
                    (sb_, sbres) = sbr.next()
                    S.op("act", lambda e, sa=sa, pga=pga: e.activation(out=sa[:], in_=pga[:], func=AF.Sigmoid),
                         reads=[rga], writes=[sares])
                    S.op("act", lambda e, sb_=sb_, pgb=pgb: e.activation(out=sb_[:], in_=pgb[:], func=AF.Sigmoid),
                         reads=[rgb], writes=[sbres])
                    S.op("dve", lambda e, sa=sa, pya=pya: e.tensor_tensor(out=sa[:], in0=pya[:], in1=sa[:], op=ALU.mult),
                         reads=[rya, sares], writes=[sares])
                    S.op("dve", lambda e, sb_=sb_, pyb=pyb: e.tensor_tensor(out=sb_[:], in0=pyb[:], in1=sb_[:], op=ALU.mult),
                         reads=[ryb, sbres], writes=[sbres])
                    S.op("pool", lambda e, sa=sa, sb_=sb_, dch=dch, ts_=ts_: e.tensor_tensor(
                        out=mg[:, dch, ts_], in0=sa[:], in1=sb_[:], op=ALU.add),
                        reads=[sares, sbres], writes=[r_mg[tg]])
            S.barrier()
            tap("mg%d" % s, mg[:, 0, :], r_mg)
            ck('PM')

            pf = Arena(AR0, AR0 + 96 * KB)
            pf2 = Arena(AR0 + 144 * KB, AR0 + AR_SZ)
            wdn = pf.alloc("wdn", [128, 22, D], BF16)
            r_wdn = Res("wdn")
            act_t = pf.alloc("actT", [128, 22, 512], BF16)
            r_act = [Res("act%d" % j) for j in range(22)]
            x1 = pf.alloc("x1", [128, 4, D], F32)
            r_x1 = [Res("x1_%d" % i) for i in range(4)]
            u2T = pf.alloc("u2T", [128, 8, 512], BF16)
            r_u2 = Res("u2T")
            xrr = Ring([(pf2.alloc("xr", [128, D], F32), Res("xr%d" % i)) for i in range(2)])
            yor = Ring([(pf2.alloc("yo", [128, D], F32), Res("yo%d" % i)) for i in range(2)])
            silr = Ring([(pf2.alloc("sil", [128, 512], F32), Res("sil%d" % i)) for i in range(1)])
            gur = Ring([((pf if i == 2 else pf2).alloc("gu", [128, 8, 2, 128], BF16), Res("gu%d" % i)) for i in range(3)])
            stat = pf2.alloc("stat", [128, 12], F32)
            mv = pf2.alloc("mv", [128, 4], F32)
            r_stat = Res("stat")
            final_res.extend([r for (_, r) in yor.items])

            def layernorm(zt, zres, gi, bi, dst, dres):
                for cch in range(2):
                    S.op("dve", lambda e, cch=cch: e.bn_stats(out=stat[:, cch * 6:(cch + 1) * 6], in_=zt[:, cch * 512:(cch + 1) * 512]),
                         reads=[zres], writes=[r_stat])
                S.op("dve", lambda e: e.bn_aggr(out=mv[:, 0:2], in_=stat[:]), reads=[r_stat], writes=[r_stat])
                S.op("act", lambda e: e.activation(out=mv[:, 2:3], in_=mv[:, 1:2], func=AF.Sqrt, bias=EPS, scale=1.0),
                     reads=[r_stat], writes=[r_stat])
                S.op("dve", lambda e: e.reciprocal(out=mv[:, 3:4], in_=mv[:, 2:3]), reads=[r_stat], writes=[r_stat])
                S.op("dve", lambda e: e.tensor_scalar(out=zt[:], in0=zt[:], scalar1=mv[:, 0:1], scalar2=mv[:, 3:4],
                                                      op0=ALU.subtract, op1=ALU.mult), reads=[zres, r_stat], writes=[zres])
                S.op("pool", lambda e: e.tensor_tensor(out=zt[:], in0=zt[:], in1=bc[gi][:], op=ALU.mult),
                     reads=[zres, r_bc[gi]], writes=[zres])
                S.op("pool", lambda e: e.tensor_tensor(out=dst, in0=zt[:], in1=bc[bi][:], op=ALU.add),
                     reads=[zres, r_bc[bi]], writes=[dres])

            ybanks = Ring([banks[0:2], banks[2:4]])
            hbanks = Ring([banks[4:6], banks[6:8]])
            def stage_wo(tg, i):
                row0 = s * SEQ + tg * 512 + i * 128
                (xr, xrres) = xrr.next()
                dma("sp", xr[:], x[row0:row0 + 128, :], writes=[xrres])
                yb_ = ybanks.next()
                for hf in range(2):
                    (py, ry) = yb_[hf]
                    for k in range(8):
                        mm(py[:], mg[:, k, tg * 512 + i * 128: tg * 512 + (i + 1) * 128], wo_t[:, k, hf * 512:(hf + 1) * 512],
                           k == 0, k == 7, [r_mg[tg], r_wo], [ry])
                zt = x1[:, i, :]
                for hf in range(2):
                    (py, ry) = yb_[hf]
                    S.op("dve", lambda e, py=py, hf=hf, i=i: e.tensor_tensor(
                        out=x1[:, i, hf * 512:(hf + 1) * 512], in0=py[:], in1=bc[4][:, hf * 512:(hf + 1) * 512], op=ALU.mult),
                        reads=[ry, r_bc[4]], writes=[r_x1[i]])
                S.op("dve", lambda e, i=i, xr=xr: e.scalar_tensor_tensor(
                    out=x1[:, i, :], in0=xr[:], scalar=ALPHA, in1=x1[:, i, :], op0=ALU.mult, op1=ALU.add),
                    reads=[xrres, r_x1[i]], writes=[r_x1[i]])
                layernorm(zt, r_x1[i], 0, 1, zt, r_x1[i])

            def stage_tr(tg, i):
                hb = hbanks.next()
                for hf in range(2):
                    (pt, pres) = hb[hf]
                    for k4 in range(4):
                        k = hf * 4 + k4
                        S.op("pe", lambda e, pt=pt, k4=k4, k=k, i=i: e.transpose(
                            pt[:, k4 * 128:(k4 + 1) * 128], x1[:, i, k * 128:(k + 1) * 128], ident[:]),
                            reads=[r_x1[i], r_const], writes=[pres])
                    for k4 in range(4):
                        k = hf * 4 + k4
                        S.op("act", lambda e, pt=pt, k4=k4, k=k, i=i, s=s: e.activation(
                            out=u2T[:, k, i * 128:(i + 1) * 128], in_=pt[:, k4 * 128:(k4 + 1) * 128], func=AF.Identity,
                            scale=modT[:, 3, k, s:s + 1], bias=modT[:, 2, k, s:s + 1]),
                            reads=[pres, r_modT], writes=[r_u2])

            def stage_ffn(tg):
                for j in range(22):
                    (gw, gwres) = gur.next()
                    dma("sp", gw[:], gu_s[j].rearrange("p (k g c) -> p k g c", k=8, g=2), reads=[r_gus], writes=[gwres])
                    hb = hbanks.next()
                    (pg, rg), (pu, ru) = hb
                    for k in range(8):
                        mm(pg[:], gw[:, k, 0, :], u2T[:, k, :], k == 0, k == 7, [gwres, r_u2], [rg])
                    for k in range(8):
                        mm(pu[:], gw[:, k, 1, :], u2T[:, k, :], k == 0, k == 7, [gwres, r_u2], [ru])
                    (sl, slres) = silr.next()
                    S.op("act", lambda e, sl=sl, pg=pg: e.activation(out=sl[:], in_=pg[:], func=AF.Silu),
                         reads=[rg], writes=[slres])
                    S.op("dve", lambda e, sl=sl, pu=pu, j=j: e.tensor_tensor(out=act_t[:, j, :], in0=pu[:], in1=sl[:], op=ALU.mult),
                         reads=[ru, slres], writes=[r_act[j]])

            def stage_down(tg, i):
                row0 = s * SEQ + tg * 512 + i * 128
                yb_ = ybanks.next()
                for hf in range(2):
                    (py, ry) = yb_[hf]
                    for j in range(22):
                        mm(py[:], act_t[:, j, i * 128:(i + 1) * 128], wdn[:, j, hf * 512:(hf + 1) * 512],
                           j == 0, j == 21, [r_act[j], r_wdn], [ry])
                (yo, yores) = yor.next()
                for hf in range(2):
                    (py, ry) = yb_[hf]
                    S.op("dve", lambda e, py=py, hf=hf, yo=yo: e.tensor_tensor(
                        out=yo[:, hf * 512:(hf + 1) * 512], in0=py[:], in1=bc[5][:, hf * 512:(hf + 1) * 512], op=ALU.mult),
                        reads=[ry, r_bc[5]], writes=[yores])
                S.op("dve", lambda e, i=i, yo=yo: e.scalar_tensor_tensor(
                    out=yo[:], in0=x1[:, i, :], scalar=ALPHA, in1=yo[:], op0=ALU.mult, op1=ALU.add),
                    reads=[r_x1[i], yores], writes=[yores])
                layernorm(yo, yores, 2, 3, yo[:], yores)
                dma("sp", out[row0:row0 + 128, :], yo[:], reads=[yores], semres=yores)

            for tg in range(4):
                for i in range(4):
                    if tg > 0:
                        stage_down(tg - 1, i)
                    stage_wo(tg, i)
                    if i > 0:
                        stage_tr(tg, i - 1)
                stage_tr(tg, 3)
                if tg == 0:
                    tap("x1_%d" % s, x1[:, 0, :], r_x1)
                    for q4 in range(4):
                        j0, j1 = q4 * 6, min(22, q4 * 6 + 6)
                        dma("pool", wdn[:, j0:j1, :], w_dn[j0 * 128:j1 * 128, :].rearrange("(k p) c -> p k c", p=128),
                            writes=[r_wdn])
                stage_ffn(tg)
            for i in range(4):
                stage_down(3, i)
            S.barrier()


    except _Stop:
        pass
    S.emit(nc, final_res=final_res + tap_res)
    return nc


def _consts():
    ident = np.eye(128, dtype=np.float32)
    rm = np.zeros((128, 128), np.float32)
    for m in range(128):
        if m % 64 < 32:
            rm[m + 32, m] = 1.0
        else:
            rm[m - 32, m] = 1.0
    k = np.arange(128)[:, None]
    q = np.arange(128)[None, :]
    diag = (q >= k).astype(np.float32)
    prev_a = (k > q).astype(np.float32)
    prev_b = (k >= q).astype(np.float32)
    ma = np.concatenate([diag, prev_a], axis=1)
    mb = np.concatenate([diag, prev_b], axis=1)
    mska = np.concatenate([ma, ma], axis=1)
    mskb = np.concatenate([mb, mb], axis=1)
    p = np.arange(128)
    invf = (np.float32(10000.0) ** (-(p % 32).astype(np.float32) / np.float32(32.0))).astype(np.float32)
    sgn = np.where(p % 64 < 32, -1.0, 1.0).astype(np.float32)
    vecs = np.stack([invf, sgn], axis=1).astype(np.float32)
    return dict(c_ident=ident, c_rm=rm, c_mska=mska, c_mskb=mskb, c_vecs=vecs)


_CACHE = {}


def _run(inputs, taps=(), stop=None):
    f = lambda a: np.ascontiguousarray(np.asarray(a))
    x = f(inputs["x"]); c = f(inputs["c"]); positions = f(inputs["positions"])
    shared = dict(
        w_ada=f(inputs["w_ada"][0]), b_ada=f(inputs["b_ada"]).reshape(1, -1), w_in=f(inputs["w_in"][0]),
        sinks=f(inputs["sinks"]).reshape(1, 16), w_a=f(inputs["w_branch_a"][0]), w_b=f(inputs["w_branch_b"][0]),
        w_o=f(inputs["w_o"][0]), ln1_g=f(inputs["ln1_g"]).reshape(1, -1), ln1_b=f(inputs["ln1_b"]).reshape(1, -1),
        ln2_g=f(inputs["ln2_g"]).reshape(1, -1), ln2_b=f(inputs["ln2_b"]).reshape(1, -1),
        w_gu=f(inputs["w_gate_up"][0]), w_dn=f(inputs["w_down"][0]))
    shared.update(_consts())
    key = (tuple(taps), stop)
    if key not in _CACHE:
        _CACHE[key] = build_program(taps, stop)
    nc = _CACHE[key]
    in_maps = []
    for i in range(NCORES):
        b0 = i * NSEQ
        m = dict(shared)
        m["x"] = f(x[b0:b0 + NSEQ].reshape(NSEQ * SEQ, D))
        m["cT"] = f(c[b0:b0 + NSEQ].T.reshape(8, 128, NSEQ).transpose(1, 0, 2))
        m["pos"] = f(positions[b0:b0 + NSEQ].astype(np.int32))
        in_maps.append(m)
    res = run_bass_kernel_spmd(nc, in_maps, core_ids=list(range(NCORES)))
    return res


def kernel(**inputs):
    res = _run(inputs)
    outs = [np.asarray(r["out"]).reshape(NSEQ, SEQ, D) for r in res.results]
    return np.concatenate(outs, axis=0).astype(np.float32)
```

```python
import math
import os
import numpy as np
LVL = int(os.environ.get('LVL', '9'))
import concourse.bass as bass
import concourse.mybir as mybir
from concourse.bass_utils import run_bass_kernel_spmd

F32 = mybir.dt.float32
BF16 = mybir.dt.bfloat16
I32 = mybir.dt.int32
ALU = mybir.AluOpType
AF = mybir.ActivationFunctionType

NCORES = 8
SEQ = 2048
D = 1024
NSEQ = 2
DFF = 2816
ALPHA = 2.0 ** 0.25
EPS = 1e-5
QA0, KA0, VA0, QB0, KB0, VB0, GA0, GB0 = 0, 1024, 1152, 1280, 2816, 4352, 5888, 6912

COMPUTE = ("pe", "act", "dve", "pool")
ENGS = ("pe", "act", "dve", "pool", "sp")


class Res:
    __slots__ = ("name", "last_w", "readers", "sem", "dma_n", "excl")

    def __init__(self, name, excl=False):
        self.name = name
        self.excl = excl
        self.last_w = None
        self.readers = []
        self.sem = None
        self.dma_n = 0


class Op:
    __slots__ = ("eng", "fn", "dma", "deps", "sig", "seq", "semres", "idx")


class Sched:
    def __init__(self):
        self.ops = []
        self.last_on = {e: None for e in ENGS}
        self.bar = {e: set() for e in ENGS}
        self.dma_since_bar = []

    def op(self, eng, fn, reads=(), writes=(), dma=False, semres=None):
        o = Op()
        o.eng, o.fn, o.dma, o.sig, o.seq = eng, fn, dma, False, 0
        o.idx = len(self.ops)
        raw, other = set(), set()
        for r in reads:
            if r.last_w is not None:
                raw.add(r.last_w)
            if r.excl:
                other.update(r.readers)
        for r in writes:
            if r.last_w is not None:
                other.add(r.last_w)
            other.update(r.readers)
        for r in reads:
            r.readers.append(o.idx)
        for r in writes:
            r.last_w = o.idx
            r.readers = []
        deps = set()
        for d in raw | other | self.bar[eng]:
            a = self.ops[d]
            if a.dma or dma or a.eng != eng:
                deps.add(d)
            elif eng != "pe" and d in raw:
                deps.add(d)
        self.bar[eng] = set()
        o.deps = sorted(deps)
        for d in o.deps:
            self.ops[d].sig = True
        if dma:
            if semres is None:
                semres = writes[0]
            o.semres = semres
            semres.dma_n += 1
            o.seq = semres.dma_n
            self.dma_since_bar.append(o.idx)
        else:
            o.semres = None
        self.ops.append(o)
        self.last_on[eng] = o.idx
        return o

    def barrier(self):
        s = set(self.dma_since_bar)
        for e in ENGS:
            if self.last_on[e] is not None:
                s.add(self.last_on[e])
        for e in ENGS:
            self.bar[e] = set(s) | self.bar[e]
        self.dma_since_bar = []

    def emit(self, nc, final_res=()):
        from contextlib import ExitStack
        cnt = {e: 0 for e in COMPUTE}
        for o in self.ops:
            if not o.dma and o.sig:
                cnt[o.eng] += 1
                o.seq = cnt[o.eng]
        with ExitStack() as es:
            esem = {e: es.enter_context(nc.semaphore("s_" + e)) for e in COMPUTE}
            nsem = 0
            for o in self.ops:
                if o.dma and o.semres.sem is None:
                    o.semres.sem = es.enter_context(nc.semaphore("d%d" % nsem))
                    nsem += 1
            self.nsem = nsem + 4
            block = es.enter_context(nc.Block())
            ops = self.ops

            def run(eng_name, eng):
                waited = {}
                for o in ops:
                    if o.eng != eng_name:
                        continue
                    need = {}
                    for d in o.deps:
                        a = ops[d]
                        if a.dma:
                            sem, val = a.semres.sem, 16 * a.seq
                        else:
                            sem, val = esem[a.eng], a.seq
                        k = id(sem)
                        if waited.get(k, 0) < val and need.get(k, (None, 0))[1] < val:
                            need[k] = (sem, val)
                    for k, (sem, val) in need.items():
                        eng.wait_ge(sem, val)
                        waited[k] = val
                    ins = o.fn(eng)
                    if o.dma:
                        ins.then_inc(o.semres.sem, 16)
                    elif o.sig:
                        ins.then_inc(esem[o.eng], 1)
                if eng_name == "sp":
                    for r in final_res:
                        if r.sem is not None and r.dma_n > 0:
                            eng.wait_ge(r.sem, 16 * r.dma_n)

            @block.tensor
            def _(e):
                run("pe", e)

            @block.scalar
            def _(e):
                run("act", e)

            @block.vector
            def _(e):
                run("dve", e)

            @block.gpsimd
            def _(e):
                run("pool", e)

            @block.sync
            def _(e):
                run("sp", e)


class Ring:
    def __init__(self, items):
        self.items = items
        self.i = 0

    def next(self):
        r = self.items[self.i % len(self.items)]
        self.i += 1
        return r


DT_BYTES = {F32: 4, BF16: 2, I32: 4}
KB = 1024


class _Stop(Exception):
    pass


def build_program(taps=(), stop=None):
    nc = bass.Bass("TRN2", target_bir_lowering=False)

    def ck(name):
        if stop == name:
            raise _Stop()
    S = Sched()
    uid = [0]

    def din(name, shape, dt=F32):
        return nc.dram_tensor(name, shape, dt, kind="ExternalInput").ap()

    x = din("x", [NSEQ * SEQ, D])
    cT = din("cT", [128, 8, NSEQ])
    pos = din("pos", [NSEQ, SEQ], I32)
    w_ada = din("w_ada", [D, 6 * D])
    b_ada = din("b_ada", [1, 6 * D])
    w_in = din("w_in", [D, 7936])
    sinks = din("sinks", [1, 16])
    w_a = din("w_a", [1024, D])
    w_b = din("w_b", [512, D])
    w_o = din("w_o", [D, D])
    ln_d = [din(n, [1, D]) for n in ("ln1_g", "ln1_b", "ln2_g", "ln2_b")]
    w_gu = din("w_gu", [D, 2 * DFF])
    w_dn = din("w_dn", [DFF, D])
    c_ident = din("c_ident", [128, 128])
    c_rm = din("c_rm", [128, 128])
    c_mska = din("c_mska", [128, 512])
    c_mskb = din("c_mskb", [128, 512])
    c_vecs = din("c_vecs", [128, 2])
    out = nc.dram_tensor("out", [NSEQ * SEQ, D], F32, kind="ExternalOutput").ap()
    tap_out = {}
    for (tn, shp) in taps:
        tap_out[tn] = nc.dram_tensor("tap_" + tn, list(shp), F32, kind="ExternalOutput").ap()

    SB_LO = 17408
    PERS = 28 * KB
    AR0 = SB_LO + PERS
    SB_TOP = 221184
    AR_SZ = SB_TOP - AR0

    class Arena:
        def __init__(self, lo, hi):
            self.lo, self.hi, self.p = lo, hi, lo

        def alloc(self, name, shape, dt):
            n = DT_BYTES[dt]
            for s_ in shape[1:]:
                n *= s_
            off = (self.p + 63) // 64 * 64
            self.p = off + n
            assert self.p <= self.hi, (name, self.p, self.hi)
            uid[0] += 1
            return nc.alloc_sbuf_tensor_at("%s_%d" % (name, uid[0]), list(shape), dt, offset=off)

    pers = Arena(SB_LO, SB_LO + PERS)
    ident = pers.alloc("ident", [128, 128], F32)
    rm = pers.alloc("rm", [128, 128], BF16)
    mska = pers.alloc("mska", [128, 2, 256], BF16)
    mskb = pers.alloc("mskb", [128, 2, 256], BF16)
    vecs = pers.alloc("vecs", [128, 2], F32)
    exps = pers.alloc("exps", [128, 16], F32)
    modT = pers.alloc("modT", [128, 4, 8, NSEQ], F32)
    cact = pers.alloc("cact", [128, 8, NSEQ], F32)
    ones_r = pers.alloc("ones_r", [1, 128], F32)
    sinkE = pers.alloc("sinkE", [128, 8], F32)
    sinkO = pers.alloc("sinkO", [128, 8], F32)
    bc = [pers.alloc("bc%d" % i, [128, D], F32) for i in range(6)]
    r_const = Res("const")
    r_modT = Res("modT")
    r_crep = Res("crep")
    r_bc = [Res("bc%d" % i) for i in range(6)]

    banks = []
    pairs = []
    for i in range(4):
        t = nc.alloc_psum_tensor("bpair%d" % i, [128, 1024], F32)
        rs = [Res("bank%d" % (2 * i + j), excl=True) for j in range(2)]
        pairs.append((t, rs))
        for j in range(2):
            banks.append((t[:, j * 512:(j + 1) * 512], rs[j]))

    def dma(q, out_ap, in_ap, reads=(), writes=(), semres=None):
        return S.op(q, lambda e: e.dma_start(out=out_ap, in_=in_ap), reads=reads, writes=writes,
                    dma=True, semres=semres)

    def mm(out_ap, lhsT, rhs, start, stop, reads, writes):
        return S.op("pe", lambda e: e.matmul(out_ap, lhsT=lhsT, rhs=rhs, start=start, stop=stop),
                    reads=reads, writes=writes)

    tap_res = []

    def tap(name, ap, reads):
        if name in tap_out:
            r = Res("tap_" + name)
            tap_res.append(r)
            dma("pool", tap_out[name], ap, reads=reads, semres=r)

    dma("sp", ident[:], c_ident, writes=[r_const])
    dma("sp", vecs[:], c_vecs, writes=[r_const])
    dma("pool", rm[:], c_rm, writes=[r_const])
    dma("pool", mska[:], c_mska.rearrange("p (h q) -> p h q", h=2), writes=[r_const])
    dma("pool", mskb[:], c_mskb.rearrange("p (h q) -> p h q", h=2), writes=[r_const])
    dma("sp", exps[:], sinks[0:1, :].partition_broadcast(128), writes=[r_const])
    dma("sp", cact[:], cT, writes=[r_const])
    for i in range(4):
        dma("sp", bc[i][:], ln_d[i][0:1, :].partition_broadcast(128), writes=[r_bc[i]])
    S.op("act", lambda e: e.activation(out=exps[:], in_=exps[:], func=AF.Exp), reads=[r_const], writes=[r_const])
    S.op("act", lambda e: e.activation(out=cact[:], in_=cact[:], func=AF.Silu), reads=[r_const], writes=[r_const])
    S.op("pool", lambda e: e.memset(ones_r[:], 1.0), writes=[r_const])
    S.op("dve", lambda e: e.memset(sinkE[:], 0.0), reads=[r_const], writes=[r_const])
    S.op("dve", lambda e: e.memset(sinkO[:], 0.0), reads=[r_const], writes=[r_const])
    S.op("dve", lambda e: e.tensor_copy(out=sinkE[64:128, :], in_=exps[:].rearrange("p (j t) -> p j t", t=2)[64:128, :, 0]),
         reads=[r_const], writes=[r_const])
    S.op("dve", lambda e: e.tensor_copy(out=sinkO[0:64, :], in_=exps[:].rearrange("p (j t) -> p j t", t=2)[0:64, :, 1]),
         reads=[r_const], writes=[r_const])
    S.barrier()
    gu_s = nc.dram_tensor("gu_s", [22, 128, 8 * 2 * 128], BF16, kind="Internal").ap()
    r_gus = Res("gus")
    for j in range(22):
        for g in range(2):
            dma("pool", gu_s[j].rearrange("p (k g c) -> p k g c", k=8, g=2)[:, :, g, :],
                w_gu[:, g * DFF + j * 128:g * DFF + (j + 1) * 128].rearrange("(k p) c -> p k c", p=128), writes=[r_gus])

    final_res = []

    def mod_columns():
        ar = Arena(AR0, AR0 + AR_SZ)
        wsl = [(ar.alloc("wada", [128, 8, 512], F32), Res("wada%d" % i)) for i in range(2)]
        bsl = [(ar.alloc("brow", [1, 512], F32), Res("brow%d" % i)) for i in range(2)]
        wr, br = Ring(wsl), Ring(bsl)
        pr = Ring(banks[0:2])
        for j, kind in enumerate((0, 1, 3, 4)):
            for half in range(2):
                c0 = kind * D + half * 512
                (wt, wres), (bt, bres) = wr.next(), br.next()
                dma("sp", wt[:], w_ada[:, c0:c0 + 512].rearrange("(k p) c -> p k c", p=128), writes=[wres])
                dma("sp", bt[:], b_ada[0:1, c0:c0 + 512], writes=[bres])
                for e4 in range(4):
                    (pt, pres) = pr.next()
                    for k in range(8):
                        mm(pt[:, 0:NSEQ], wt[:, k, e4 * 128:(e4 + 1) * 128], cact[:, k, :], k == 0, False,
                           [wres, r_const], [pres])
                    mm(pt[:, 0:NSEQ], bt[0:1, e4 * 128:(e4 + 1) * 128], ones_r[0:1, 0:NSEQ], False, True,
                       [bres, r_const], [pres])
                    kk = half * 4 + e4
                    addv = 1.0 if kind in (1, 4) else 0.0
                    S.op("dve", lambda e, pt=pt, j=j, kk=kk, addv=addv: e.tensor_scalar(
                        out=modT[:, j, kk, :], in0=pt[:, 0:NSEQ], scalar1=addv, scalar2=None, op0=ALU.add),
                        reads=[pres], writes=[r_modT])

    def mod_gate_rows(s):
        ar = Arena(AR0, AR0 + AR_SZ)
        wsl = [(ar.alloc("wadg", [128, 8, 512], F32), Res("wadg%d" % i)) for i in range(2)]
        bsl = [(ar.alloc("browg", [1, 512], F32), Res("browg%d" % i)) for i in range(2)]
        crep = ar.alloc("crep", [128, 8, 128], F32)
        wr, br = Ring(wsl), Ring(bsl)
        pr = Ring(banks[0:2])
        for k in range(8):
            S.op("dve", lambda e, k=k: e.tensor_scalar(out=crep[:, k, :], in0=ident[:], scalar1=0.0,
                                                       scalar2=cact[:, k, s:s + 1], op0=ALU.mult, op1=ALU.add),
                 reads=[r_const], writes=[r_crep])
        for gi, kind in enumerate((2, 5)):
            for half in range(2):
                c0 = kind * D + half * 512
                (wt, wres), (bt, bres) = wr.next(), br.next()
                dma("sp", wt[:], w_ada[:, c0:c0 + 512].rearrange("(k p) c -> p k c", p=128), writes=[wres])
                dma("sp", bt[:], b_ada[0:1, c0:c0 + 512], writes=[bres])
                (pt, pres) = pr.next()
                for k in range(8):
                    mm(pt[:], crep[:, k, :], wt[:, k, :], k == 0, False, [wres, r_crep], [pres])
                mm(pt[:], ones_r[0:1, :], bt[0:1, :], False, True, [bres, r_const], [pres])
                S.op("dve", lambda e, pt=pt, gi=gi, half=half: e.tensor_scalar(
                    out=bc[4 + gi][:, half * 512:(half + 1) * 512], in0=pt[:], scalar1=1.0, scalar2=None, op0=ALU.add),
                    reads=[pres], writes=[r_bc[4 + gi]])

    try:
        ck('const')
        mod_columns()
        S.barrier()
        ck('mod')

        for s in range(int(os.environ.get('NS', NSEQ))):
            mod_gate_rows(s)
            S.barrier()
            ck('gate')
            fx = Arena(AR0, AR0 + 96 * KB)
            uT = fx.alloc("uT", [128, 8, SEQ], BF16)
            cosT = fx.alloc("cosT", [128, SEQ], F32)
            sinS = fx.alloc("sinS", [128, SEQ], F32)
            oa_off = (fx.p + 63) // 64 * 64
            OA = fx.alloc("OA", [128, 8, SEQ], BF16)
            OB = fx.alloc("OB", [128, 4, SEQ], BF16)
            r_uT = [Res("uT%d" % t) for t in range(4)]
            r_cs = Res("cs")
            r_OA = [Res("OA%d" % j) for j in range(8)]
            r_OB = [Res("OB%d" % j) for j in range(4)]
            ub = Arena(AR0 + 96 * KB, AR0 + AR_SZ)
            wsl = Ring([(ub.alloc("wu", [128, 8, 3, 128], BF16), Res("wu%d" % i)) for i in range(2)])
            QTr = Ring([(ub.alloc("QT", [128, SEQ], BF16), Res("QT%d" % i)) for i in range(2)])
            KTr_items = [(ub.alloc("KT", [128, SEQ], BF16), Res("KT%d" % i)) for i in range(2)]
            KTr = Ring(KTr_items)
            VOr_items = [(ub.alloc("VO", [128, 16, 2, 128], BF16), Res("VO%d" % i)) for i in range(2)]
            VOr = Ring(VOr_items)
            acc_mark = ub.p
            accO = ub.alloc("accO", [128, SEQ], F32)
            accD = ub.alloc("accD", [128, SEQ], F32)
            r_accs = [Res("acc%d" % k) for k in range(4)]
            Pr = [(ub.alloc("P", [128, 2, 256], BF16), Res("P%d" % i)) for i in range(4)]
            e_off = (ub.p + 63) // 64 * 64
            Er = Ring([(ub.alloc("E", [128, 2, 256], BF16), Res("E%d" % i)) for i in range(2)])
            xbr = Ring([(ub.alloc("xb", [128, 512], BF16), Res("xb%d" % i)) for i in range(2)])
            t1r = Ring([(ub.alloc("t1", [128, 512], F32), Res("t1_%d" % i)) for i in range(2)])
            t2b = nc.alloc_sbuf_tensor_at("t2b_%d" % s, [128, 512], F32, offset=e_off)
            t2r = Ring([(ub.alloc("t2", [128, 512], F32), [Res("t2_0")]), (t2b, [Er.items[0][1], Er.items[1][1]])])
            tr = Arena(acc_mark, acc_mark + 16 * KB)
            posi = tr.alloc("posi", [128, SEQ], I32)
            xsr = Ring([(tr.alloc("xs", [128, D], F32), Res("xs%d" % i)) for i in range(2)])
            r_posi = Res("posi")
            dma("sp", posi[:], pos[s:s + 1, :].partition_broadcast(128), writes=[r_posi])
            tmpa = nc.alloc_sbuf_tensor_at("tmpa_%d" % s, [128, SEQ], F32, offset=oa_off)
            I2P = 1.0 / (2 * math.pi)
            rr = [r_posi, r_cs]
            S.op("dve", lambda e: e.tensor_copy(out=cosT[:], in_=posi[:]), reads=rr, writes=rr)
            S.op("dve", lambda e: e.tensor_scalar(out=cosT[:], in0=cosT[:], scalar1=vecs[:, 0:1], scalar2=None, op0=ALU.mult),
                 reads=rr + [r_const], writes=rr)
            S.op("dve", lambda e: e.tensor_scalar(out=sinS[:], in0=cosT[:], scalar1=I2P, scalar2=None, op0=ALU.mult), reads=rr, writes=rr)
            S.op("dve", lambda e: e.tensor_copy(out=posi[:], in_=sinS[:]), reads=rr, writes=rr)
            S.op("dve", lambda e: e.tensor_copy(out=sinS[:], in_=posi[:]), reads=rr, writes=rr)
            S.op("dve", lambda e: e.scalar_tensor_tensor(out=sinS[:], in0=sinS[:], scalar=-2 * math.pi, in1=cosT[:],
                                                         op0=ALU.mult, op1=ALU.add), reads=rr, writes=rr)
            S.op("dve", lambda e: e.tensor_scalar(out=sinS[:], in0=sinS[:], scalar1=math.pi, scalar2=-math.pi,
                                                  op0=ALU.min, op1=ALU.max), reads=rr, writes=rr)
            S.op("dve", lambda e: e.tensor_scalar(out=tmpa[:], in0=cosT[:], scalar1=0.5 * math.pi, scalar2=None, op0=ALU.add),
                 reads=rr, writes=rr)
            S.op("dve", lambda e: e.tensor_scalar(out=cosT[:], in0=tmpa[:], scalar1=I2P, scalar2=None, op0=ALU.mult), reads=rr, writes=rr)
            S.op("dve", lambda e: e.tensor_copy(out=posi[:], in_=cosT[:]), reads=rr, writes=rr)
            S.op("dve", lambda e: e.tensor_copy(out=cosT[:], in_=posi[:]), reads=rr, writes=rr)
            S.op("dve", lambda e: e.scalar_tensor_tensor(out=cosT[:], in0=cosT[:], scalar=-2 * math.pi, in1=tmpa[:],
                                                         op0=ALU.mult, op1=ALU.add), reads=rr, writes=rr)
            S.op("dve", lambda e: e.tensor_scalar(out=cosT[:], in0=cosT[:], scalar1=math.pi, scalar2=-math.pi,
                                                  op0=ALU.min, op1=ALU.max), reads=rr, writes=rr)
            S.op("act", lambda e: e.activation(out=sinS[:], in_=sinS[:], func=AF.Sin), reads=rr, writes=rr)
            S.op("act", lambda e: e.activation(out=cosT[:], in_=cosT[:], func=AF.Sin), reads=rr, writes=rr)
            S.op("dve", lambda e: e.tensor_scalar(out=sinS[:], in0=sinS[:], scalar1=vecs[:, 1:2], scalar2=None, op0=ALU.mult),
                 reads=rr + [r_const], writes=rr)
            for i in range(16):
                (xt, xres) = xsr.next()
                dma("sp", xt[:], x[s * SEQ + i * 128: s * SEQ + (i + 1) * 128, :], writes=[xres])
                for hf in range(2):
                    (pt, pres) = banks[hf + 2 * (i % 2)]
                    for k4 in range(4):
                        k = hf * 4 + k4
                        S.op("pe", lambda e, pt=pt, k4=k4, xt=xt, k=k: e.transpose(
                            pt[:, k4 * 128:(k4 + 1) * 128], xt[:, k * 128:(k + 1) * 128], ident[:]),
                            reads=[xres, r_const], writes=[pres])
                    for k4 in range(4):
                        k = hf * 4 + k4
                        S.op("act", lambda e, pt=pt, k4=k4, k=k, i=i, s=s: e.activation(
                            out=uT[:, k, i * 128:(i + 1) * 128], in_=pt[:, k4 * 128:(k4 + 1) * 128], func=AF.Identity,
                            scale=modT[:, 1, k, s:s + 1], bias=modT[:, 0, k, s:s + 1]),
                            reads=[pres, r_modT], writes=[r_uT[i // 4]])
            S.barrier()
            tap("uT%d" % s, uT[:, 0, :], r_uT)
            tap("cos%d" % s, cosT[:], [r_cs])
            tap("sin%d" % s, sinS[:], [r_cs])
            ck('ut')

            proj_banks = Ring(banks[0:2])
            rot_banks = Ring(banks[2:4])
            s_pairs = Ring(pairs[2:4])
            pv_banks = Ring(banks[2:4])

            def load_w(cols):
                (wt, wres) = wsl.next()
                for pi, c0 in enumerate(cols):
                    dma("pool", wt[:, :, pi, :], w_in[:, c0:c0 + 128].rearrange("(k p) c -> p k c", p=128), writes=[wres])
                return wt, wres

            def proj_rope(wt, wres, part, dest, dres, r, between=None):
                def stage_b(st):
                    (tg, xb, xbres, t1, t1res) = st
                    (t2, t2res) = t2r.next()
                    (rt, rres) = rot_banks.next()
                    mm(rt[:], rm[:], xb[:], True, True, [xbres, r_const], [rres])
                    S.op("dve", lambda e, t2=t2, rt=rt, tg=tg: e.tensor_tensor(
                        out=t2[:], in0=rt[:], in1=sinS[:, tg * 512:(tg + 1) * 512], op=ALU.mult),
                        reads=[rres, r_cs], writes=t2res)
                    n = 512 // r
                    dv = dest[:].rearrange("p (c i) -> p c i", c=r)[:, :, tg * n:(tg + 1) * n]
                    a1 = t1[:].rearrange("p (i c) -> p c i", c=r)
                    a2 = t2[:].rearrange("p (i c) -> p c i", c=r)
                    S.op("pool", lambda e, dv=dv, a1=a1, a2=a2: e.tensor_tensor(out=dv, in0=a1, in1=a2, op=ALU.add),
                         reads=[t1res] + t2res, writes=[dres])

                pend = None
                for tg in range(4):
                    (pt, pres) = proj_banks.next()
                    for k in range(8):
                        mm(pt[:], wt[:, k, part, :], uT[:, k, tg * 512:(tg + 1) * 512], k == 0, k == 7,
                           [wres, r_uT[tg]], [pres])
                    (xb, xbres) = xbr.next()
                    (t1, t1res) = t1r.next()
                    S.op("act", lambda e, xb=xb, pt=pt: e.activation(out=xb[:], in_=pt[:], func=AF.Copy),
                         reads=[pres], writes=[xbres])
                    S.op("dve", lambda e, t1=t1, pt=pt, tg=tg: e.tensor_tensor(
                        out=t1[:], in0=pt[:], in1=cosT[:, tg * 512:(tg + 1) * 512], op=ALU.mult),
                        reads=[pres, r_cs], writes=[t1res])
                    if pend is not None:
                        stage_b(pend)
                    pend = (tg, xb, xbres, t1, t1res)
                    if between:
                        between.pop(0)()
                stage_b(pend)
                while between:
                    between.pop(0)()

            def proj_v(wt, wres, part, vo, vores, r, ncols_layout):
                nblk = 16 // r
                for c in range(r):
                    for kb in range(nblk):
                        (pt, pres) = proj_banks.next()
                        for k in range(8):
                            lt = uT[:, k, :].rearrange("p (i c) -> p c i", c=r)[:, c, kb * 128:(kb + 1) * 128]
                            mm(pt[:, 0:128], lt, wt[:, k, part, :], k == 0, k == 7, [wres] + r_uT, [pres])
                        for (vt, vr, cmap) in ncols_layout:
                            for (h, d0, s0) in cmap:
                                S.op("act", lambda e, vt=vt, pt=pt, h=h, d0=d0, s0=s0, c=c, kb=kb: e.activation(
                                    out=vt[:, c * nblk + kb, h, d0:d0 + 64], in_=pt[:, s0:s0 + 64], func=AF.Copy),
                                    reads=[pres], writes=[vr])

            def attention(qt, qres, kt, kres, vo, vores, r, msk, first, tog, sinkj=None, fin_cb=None):
                nblk = 16 // r
                clen = SEQ // r
                nT = (nblk + 1) // 2
                steps = [(c, m) for c in range(r) for m in range(nT)]

                def emit_S(c, m):
                    lst = []
                    for kb in range(2 * m, min(2 * m + 2, nblk)):
                        nq = 256 if kb + 1 < nblk else 128
                        (st, sresl) = s_pairs.next()
                        sv = st[:].rearrange("p (h q) -> p h q", h=2)
                        for h in range(2):
                            mm(sv[:, h, 0:nq], kt[64 * h:64 * h + 64, c * clen + kb * 128: c * clen + (kb + 1) * 128],
                               qt[64 * h:64 * h + 64, c * clen + kb * 128: c * clen + kb * 128 + nq], True, True,
                               [kres, qres], [sresl[h]])
                        lst.append((kb, nq, sv, sresl))
                    return lst

                def emit_exp(c, item):
                    (kb, nq, sv, sresl) = item
                    (et, eres) = Er.next()
                    S.op("act", lambda e, et=et, sv=sv, nq=nq: e.activation(
                        out=et[:, :, 0:nq], in_=sv[:, :, 0:nq], func=AF.Exp, scale=0.125),
                        reads=sresl, writes=[eres])
                    return (et, eres)

                def emit_mask(c, item, e_, eng):
                    (kb, nq, sv, sresl) = item
                    (et, eres) = e_
                    (pt_, pres_) = Pr[(c * nblk + kb) % 4]
                    S.op(eng, lambda e, pt_=pt_, et=et, nq=nq: e.tensor_tensor(
                        out=pt_[:, :, 0:nq], in0=et[:, :, 0:nq], in1=msk[:, :, 0:nq], op=ALU.mult),
                        reads=[eres, r_const], writes=[pres_])

                def emit_PV(c, m):
                    W = min(256, nblk * 128)
                    (pv, pvres) = pv_banks.next()
                    pvv = pv[:].rearrange("p (h q) -> p h q", h=2)
                    g0 = c * nblk
                    for h in range(2):
                        has2 = (2 * m - 1 >= 0)
                        has3 = (2 * m + 1 < nblk)
                        P0 = Pr[(g0 + 2 * m) % 4]
                        mm(pvv[:, h, 0:W], vo[:, g0 + 2 * m, h, :], P0[0][:, h, 0:W], True,
                           not (has2 or has3), [vores, P0[1]], [pvres])
                        if has2:
                            P2 = Pr[(g0 + 2 * m - 1) % 4]
                            mm(pvv[:, h, 0:128], vo[:, g0 + 2 * m - 1, h, :], P2[0][:, h, 128:256],
                               False, not has3, [vores, P2[1]], [pvres])
                        if has3:
                            P3 = Pr[(g0 + 2 * m + 1) % 4]
                            mm(pvv[:, h, 128:256], vo[:, g0 + 2 * m + 1, h, :], P3[0][:, h, 0:128],
                               False, True, [vores, P3[1]], [pvres])

                    def nat(t_, lo, hi):
                        return t_[lo:hi, :].rearrange("p (i c) -> p c i", c=r)[:, c, 256 * m:256 * m + W]
                    prs = [(nat(accO, 0, 128), pvv[:, 0, 0:W]), (nat(accD, 0, 128), pvv[:, 1, 0:W])]

                    lo = r * 256 * m + c
                    hi = lo + r * (W - 1)
                    rch = [r_accs[k] for k in range(lo // 512, hi // 512 + 1)]

                    def do_acc():
                        for hi_, (dst, src) in enumerate(prs):
                            if first and sinkj is not None:
                                sc = (sinkE if hi_ == 0 else sinkO)[:, sinkj:sinkj + 1]
                                S.op("dve", lambda e, dst=dst, src=src, sc=sc: e.tensor_scalar(
                                    out=dst, in0=src, scalar1=sc, scalar2=None, op0=ALU.add),
                                    reads=[pvres, r_const], writes=rch)
                            elif first:
                                S.op("dve", lambda e, dst=dst, src=src: e.tensor_copy(out=dst, in_=src),
                                     reads=[pvres], writes=rch)
                            else:
                                S.op("dve", lambda e, dst=dst, src=src: e.tensor_tensor(out=dst, in0=src, in1=dst, op=ALU.add),
                                     reads=[pvres] + rch, writes=rch)
                    return do_acc

                pend = None
                for t in range(len(steps) + 1):
                    cur = None
                    late = None
                    if t < len(steps):
                        c, m = steps[t]
                        lst = emit_S(c, m)
                        cur = (c, m)
                        e0 = emit_exp(c, lst[0])
                        emit_mask(c, lst[0], e0, "pool")
                        if len(lst) > 1:
                            e1 = emit_exp(c, lst[1])
                            late = (c, lst[1], e1)
                    acc_fn = None
                    if pend is not None:
                        acc_fn = emit_PV(pend[0], pend[1])
                    if late is not None:
                        emit_mask(late[0], late[1], late[2], "dve")
                    if acc_fn is not None:
                        acc_fn()
                        if fin_cb is not None and pend[1] % 2 == 1:
                            fin_cb(pend[1] // 2)
                    pend = cur

            fin_ring = Ring([(t_, [r_]) for (t_, r_) in t1r.items] + list(t2r.items))

            def finalize_chunk(dest, dres, ch, mul_eng, ring=None):
                cs_ = slice(ch * 512, (ch + 1) * 512)
                ra = [r_accs[ch]]
                (tmp, tres) = (ring or fin_ring).next()
                S.op("dve", lambda e, tmp=tmp, cs_=cs_: e.tensor_copy(out=tmp[0:64, :], in_=accO[64:128, cs_]),
                     reads=ra, writes=tres)
                S.op("dve", lambda e, tmp=tmp, cs_=cs_: e.tensor_copy(out=tmp[64:128, :], in_=accD[0:64, cs_]),
                     reads=ra, writes=tres)
                S.op("act", lambda e, tmp=tmp: e.activation(out=tmp[:], in_=tmp[:], func=AF.Ln), reads=tres, writes=tres)
                S.op("act", lambda e, tmp=tmp: e.activation(out=tmp[:], in_=tmp[:], func=AF.Exp, scale=-1.0), reads=tres, writes=tres)
                S.op(mul_eng, lambda e, tmp=tmp, cs_=cs_: e.tensor_tensor(
                    out=dest[0:64, cs_], in0=accO[0:64, cs_], in1=tmp[0:64, :], op=ALU.mult),
                    reads=ra + tres, writes=[dres])
                S.op(mul_eng, lambda e, tmp=tmp, cs_=cs_: e.tensor_tensor(
                    out=dest[64:128, cs_], in0=accD[64:128, cs_], in1=tmp[64:128, :], op=ALU.mult),
                    reads=ra + tres, writes=[dres])

            def finalize(dest, dres, sink_heads, mul_eng):
                for ch in range(4):
                    finalize_chunk(dest, dres, ch, mul_eng)

            tog = [0]
            for (vt, vr) in VOr_items:
                S.op("pool", lambda e, vt=vt: e.memset(vt[:], 1.0), writes=[vr])
            b_units = [(p, g, r) for p in range(4) for g, r in enumerate((1, 4, 16))]

            def bcols(p, g):
                hc = (g * 8 + 2 * p) * 64
                return [QB0 + hc, KB0 + hc, VB0 + hc]
            w_next = load_w(bcols(0, 0))
            pend_fin = [None]

            def flush_fin():
                if pend_fin[0] is not None:
                    finalize(*pend_fin[0])
                    pend_fin[0] = None
            for ui, (p, g, r) in enumerate(b_units):
                wt, wres = w_next
                (qt, qres) = QTr.next()
                (kt, kres) = KTr.next()
                (vo, vores) = VOr.next()
                proj_rope(wt, wres, 0, qt, qres, r)
                proj_rope(wt, wres, 1, kt, kres, r)
                ck('b1')
                proj_v(wt, wres, 2, vo, vores, r, [(vo, vores, [(0, 0, 0), (1, 64, 64)])])
                ck('b1v')
                if ui + 1 < len(b_units):
                    w_next = load_w(bcols(b_units[ui + 1][0], b_units[ui + 1][1]))
                else:
                    w_next = load_w([KA0, VA0])
                flush_fin()
                attention(qt, qres, kt, kres, vo, vores, r, mskb, g == 0, tog)
                ck('b1a')
                if g == 2:
                    pend_fin[0] = (OB[:, p, :], r_OB[p], None, "pool")
                    ck('bfin')
            tap("OB%d" % s, OB[:, 0, :], r_OB)
            ck('B')
            KAd = KTr_items
            VOA = VOr_items
            wt, wres = w_next
            w_next = load_w([QA0])
            (KAn, r_KAn) = QTr.next()
            proj_rope(wt, wres, 0, KAn, r_KAn, 1)
            for kv in range(2):
                (kd, kdres) = KAd[kv]
                for h in range(2):
                    S.op("dve", lambda e, kd=kd, kv=kv, h=h: e.tensor_copy(
                        out=kd[64 * h:64 * h + 64, :], in_=KAn[64 * kv:64 * kv + 64, :]),
                        reads=[r_KAn], writes=[kdres])
            proj_v(wt, wres, 1, None, None, 1,
                   [(VOA[0][0], VOA[0][1], [(0, 0, 0), (1, 64, 0)]), (VOA[1][0], VOA[1][1], [(0, 0, 64), (1, 64, 64)])])
            att_ring = Ring([(t_, [r_]) for (t_, r_) in t1r.items] + [t2r.items[0]])
            for j in range(8):
                wt, wres = w_next
                (qt, qres) = QTr.next()
                proj_rope(wt, wres, 0, qt, qres, 1)
                if j + 1 < 8:
                    w_next = load_w([QA0 + (j + 1) * 128])
                kv = j // 4
                flush_fin()
                attention(qt, qres, KAd[kv][0], KAd[kv][1], VOA[kv][0], VOA[kv][1], 1, mska, True, tog, sinkj=j,
                          fin_cb=(lambda ch, j=j: finalize_chunk(OA[:, j, :], r_OA[j], ch, "pool", att_ring)))
            S.barrier()
            tap("OA%d" % s, OA[:, 0, :], r_OA)
            ck('A')

            mg = nc.alloc_sbuf_tensor_at("merged_%d" % s, [128, 8, SEQ], BF16, offset=AR0 + 96 * KB)
            r_mg = [Res("mg%d" % t) for t in range(4)]
            wo_t = nc.alloc_sbuf_tensor_at("wo_%d" % s, [128, 8, D], BF16, offset=AR0 + 128 * KB)
            r_wo = Res("wo")
            pm = Arena(AR0 + 144 * KB, AR0 + AR_SZ)
            wmr = Ring([(pm.alloc("wm", [128, 28, 128], BF16), Res("wm%d" % i)) for i in range(2)])
            sar = Ring([(pm.alloc("sa", [128, 512], F32), Res("sa%d" % i)) for i in range(2)])
            sbr = Ring([(pm.alloc("sb", [128, 512], F32), Res("sb%d" % i)) for i in range(2)])
            for hf in range(2):
                dma("pool", wo_t[:, :, hf * 512:(hf + 1) * 512],
                    w_o[:, hf * 512:(hf + 1) * 512].rearrange("(k p) c -> p k c", p=128), writes=[r_wo])
            bring = Ring([banks[0:4], banks[4:8]])
            def load_wm(dch):
                (wm, wmres) = wmr.next()
                cs_ = slice(dch * 128, (dch + 1) * 128)
                dma("pool", wm[:, 0:8, :], w_a[:, cs_].rearrange("(k p) c -> p k c", p=128), writes=[wmres])
                dma("pool", wm[:, 8:12, :], w_b[:, cs_].rearrange("(k p) c -> p k c", p=128), writes=[wmres])
                dma("pool", wm[:, 12:20, :], w_in[:, GA0 + dch * 128:GA0 + (dch + 1) * 128].rearrange("(k p) c -> p k c", p=128),
                    writes=[wmres])
                dma("pool", wm[:, 20:28, :], w_in[:, GB0 + dch * 128:GB0 + (dch + 1) * 128].rearrange("(k p) c -> p k c", p=128),
                    writes=[wmres])
                return wm, wmres
            wm_next = load_wm(0)
            for dch in range(8):
                (wm, wmres) = wm_next
                if dch + 1 < 8:
                    wm_next = load_wm(dch + 1)
                for tg in range(4):
                    bk = bring.next()
                    ts_ = slice(tg * 512, (tg + 1) * 512)
                    (pya, rya), (pyb, ryb), (pga, rga), (pgb, rgb) = bk
                    for k in range(8):
                        mm(pya[:], wm[:, k, :], OA[:, k, ts_], k == 0, k == 7, [wmres, r_OA[k]], [rya])
                    for k in range(4):
                        mm(pyb[:], wm[:, 8 + k, :], OB[:, k, ts_], k == 0, k == 3, [wmres, r_OB[k]], [ryb])
                    for k in range(8):
                        mm(pga[:], wm[:, 12 + k, :], uT[:, k, ts_], k == 0, k == 7, [wmres, r_uT[tg]], [rga])
                    for k in range(8):
                        mm(pgb[:], wm[:, 20 + k, :], uT[:, k, ts_], k == 0, k == 7, [wmres, r_uT[tg]], [rgb])
                    (sa, sares) = sar.next()
                    (sb_, sbres) = sbr.next()
                    S.op("act", lambda e, sa=sa, pga=pga: e.activation(out=sa[:], in_=pga[:], func=AF.Sigmoid),
                         reads=[rga], writes=[sares])
                    S.op("act", lambda e, sb_=sb_, pgb=pgb: e.activation(out=sb_[:], in_=pgb[:], func=AF.Sigmoid),
                         reads=[rgb], writes=[sbres])
                    S.op("dve", lambda e, sa=sa, pya=pya: e.tensor_tensor(out=sa[:], in0=pya[:], in1=sa[:], op=ALU.mult),
                         reads=[rya, sares], writes=[sares])
                    S.op("dve", lambda e, sb_=sb_, pyb=pyb: e.tensor_tensor(out=sb_[:], in0=pyb[:], in1=sb_[:], op=ALU.mult),
                         reads=[ryb, sbres], writes=[sbres])
                    S.op("pool", lambda e, sa=sa, sb_=sb_, dch=dch, ts_=ts_: e.tensor_tensor(
                        out=mg[:, dch, ts_], in0=sa[:], in1=sb_[:], op=ALU.add),
                        reads=[sares, sbres], writes=[r_mg[tg]])
            S.barrier()
            tap("mg%d" % s, mg[:, 0, :], r_mg)
            ck('PM')

            pf = Arena(AR0, AR0 + 96 * KB)
            pf2 = Arena(AR0 + 144 * KB, AR0 + AR_SZ)
            wdn = pf.alloc("wdn", [128, 22, D], BF16)
            r_wdn = Res("wdn")
            act_t = pf.alloc("actT", [128, 22, 512], BF16)
            r_act = [Res("act%d" % j) for j in range(22)]
            x1 = pf.alloc("x1", [128, 4, D], F32)
            r_x1 = [Res("x1_%d" % i) for i in range(4)]
            u2T = pf.alloc("u2T", [128, 8, 512], BF16)
            r_u2 = Res("u2T")
            xrr = Ring([(pf2.alloc("xr", [128, D], F32), Res("xr%d" % i)) for i in range(2)])
            yor = Ring([(pf2.alloc("yo", [128, D], F32), Res("yo%d" % i)) for i in range(2)])
            silr = Ring([(pf2.alloc("sil", [128, 512], F32), Res("sil%d" % i)) for i in range(1)])
            gur = Ring([((pf if i == 2 else pf2).alloc("gu", [128, 8, 2, 128], BF16), Res("gu%d" % i)) for i in range(3)])
            stat = pf2.alloc("stat", [128, 12], F32)
            mv = pf2.alloc("mv", [128, 4], F32)
            r_stat = Res("stat")
            final_res.extend([r for (_, r) in yor.items])

            def layernorm(zt, zres, gi, bi, dst, dres):
                for cch in range(2):
                    S.op("dve", lambda e, cch=cch: e.bn_stats(out=stat[:, cch * 6:(cch + 1) * 6], in_=zt[:, cch * 512:(cch + 1) * 512]),
                         reads=[zres], writes=[r_stat])
                S.op("dve", lambda e: e.bn_aggr(out=mv[:, 0:2], in_=stat[:]), reads=[r_stat], writes=[r_stat])
                S.op("act", lambda e: e.activation(out=mv[:, 2:3], in_=mv[:, 1:2], func=AF.Sqrt, bias=EPS, scale=1.0),
                     reads=[r_stat], writes=[r_stat])
                S.op("dve", lambda e: e.reciprocal(out=mv[:, 3:4], in_=mv[:, 2:3]), reads=[r_stat], writes=[r_stat])
                S.op("dve", lambda e: e.tensor_scalar(out=zt[:], in0=zt[:], scalar1=mv[:, 0:1], scalar2=mv[:, 3:4],
                                                      op0=ALU.subtract, op1=ALU.mult), reads=[zres, r_stat], writes=[zres])
                S.op("pool", lambda e: e.tensor_tensor(out=zt[:], in0=zt[:], in1=bc[gi][:], op=ALU.mult),
                     reads=[zres, r_bc[gi]], writes=[zres])
                S.op("pool", lambda e: e.tensor_tensor(out=dst, in0=zt[:], in1=bc[bi][:], op=ALU.add),
                     reads=[zres, r_bc[bi]], writes=[dres])

            ybanks = Ring([banks[0:2], banks[2:4]])
            hbanks = Ring([banks[4:6], banks[6:8]])
            def stage_wo(tg, i):
                row0 = s * SEQ + tg * 512 + i * 128
                (xr, xrres) = xrr.next()
                dma("sp", xr[:], x[row0:row0 + 128, :], writes=[xrres])
                yb_ = ybanks.next()
                for hf in range(2):
                    (py, ry) = yb_[hf]
                    for k in range(8):
                        mm(py[:], mg[:, k, tg * 512 + i * 128: tg * 512 + (i + 1) * 128], wo_t[:, k, hf * 512:(hf + 1) * 512],
                           k == 0, k == 7, [r_mg[tg], r_wo], [ry])
                zt = x1[:, i, :]
                for hf in range(2):
                    (py, ry) = yb_[hf]
                    S.op("dve", lambda e, py=py, hf=hf, i=i: e.tensor_tensor(
                        out=x1[:, i, hf * 512:(hf + 1) * 512], in0=py[:], in1=bc[4][:, hf * 512:(hf + 1) * 512], op=ALU.mult),
                        reads=[ry, r_bc[4]], writes=[r_x1[i]])
                S.op("dve", lambda e, i=i, xr=xr: e.scalar_tensor_tensor(
                    out=x1[:, i, :], in0=xr[:], scalar=ALPHA, in1=x1[:, i, :], op0=ALU.mult, op1=ALU.add),
                    reads=[xrres, r_x1[i]], writes=[r_x1[i]])
                layernorm(zt, r_x1[i], 0, 1, zt, r_x1[i])

            def stage_tr(tg, i):
                hb = hbanks.next()
                for hf in range(2):
                    (pt, pres) = hb[hf]
                    for k4 in range(4):
                        k = hf * 4 + k4
                        S.op("pe", lambda e, pt=pt, k4=k4, k=k, i=i: e.transpose(
                            pt[:, k4 * 128:(k4 + 1) * 128], x1[:, i, k * 128:(k + 1) * 128], ident[:]),
                            reads=[r_x1[i], r_const], writes=[pres])
                    for k4 in range(4):
                        k = hf * 4 + k4
                        S.op("act", lambda e, pt=pt, k4=k4, k=k, i=i, s=s: e.activation(
                            out=u2T[:, k, i * 128:(i + 1) * 128], in_=pt[:, k4 * 128:(k4 + 1) * 128], func=AF.Identity,
                            scale=modT[:, 3, k, s:s + 1], bias=modT[:, 2, k, s:s + 1]),
                            reads=[pres, r_modT], writes=[r_u2])

            def stage_ffn(tg):
                for j in range(22):
                    (gw, gwres) = gur.next()
                    dma("sp", gw[:], gu_s[j].rearrange("p (k g c) -> p k g c", k=8, g=2), reads=[r_gus], writes=[gwres])
                    hb = hbanks.next()
                    (pg, rg), (pu, ru) = hb
                    for k in range(8):
                        mm(pg[:], gw[:, k, 0, :], u2T[:, k, :], k == 0, k == 7, [gwres, r_u2], [rg])
                    for k in range(8):
                        mm(pu[:], gw[:, k, 1, :], u2T[:, k, :], k == 0, k == 7, [gwres, r_u2], [ru])
                    (sl, slres) = silr.next()
                    S.op("act", lambda e, sl=sl, pg=pg: e.activation(out=sl[:], in_=pg[:], func=AF.Silu),
                         reads=[rg], writes=[slres])
                    S.op("dve", lambda e, sl=sl, pu=pu, j=j: e.tensor_tensor(out=act_t[:, j, :], in0=pu[:], in1=sl[:], op=ALU.mult),
                         reads=[ru, slres], writes=[r_act[j]])

            def stage_down(tg, i):
                row0 = s * SEQ + tg * 512 + i * 128
                yb_ = ybanks.next()
                for hf in range(2):
                    (py, ry) = yb_[hf]
                    for j in range(22):
                        mm(py[:], act_t[:, j, i * 128:(i + 1) * 128], wdn[:, j, hf * 512:(hf + 1) * 512],
                           j == 0, j == 21, [r_act[j], r_wdn], [ry])
                (yo, yores) = yor.next()
                for hf in range(2):
                    (py, ry) = yb_[hf]
                    S.op("dve", lambda e, py=py, hf=hf, yo=yo: e.tensor_tensor(
                        out=yo[:, hf * 512:(hf + 1) * 512], in0=py[:], in1=bc[5][:, hf * 512:(hf + 1) * 512], op=ALU.mult),
                        reads=[ry, r_bc[5]], writes=[yores])
                S.op("dve", lambda e, i=i, yo=yo: e.scalar_tensor_tensor(
                    out=yo[:], in0=x1[:, i, :], scalar=ALPHA, in1=yo[:], op0=ALU.mult, op1=ALU.add),
                    reads=[r_x1[i], yores], writes=[yores])
                layernorm(yo, yores, 2, 3, yo[:], yores)
                dma("sp", out[row0:row0 + 128, :], yo[:], reads=[yores], semres=yores)

            for tg in range(4):
                for i in range(4):
                    if tg > 0:
                        stage_down(tg - 1, i)
                    stage_wo(tg, i)
                    if i > 0:
                        stage_tr(tg, i - 1)
                stage_tr(tg, 3)
                if tg == 0:
                    tap("x1_%d" % s, x1[:, 0, :], r_x1)
                    for q4 in range(4):
                        j0, j1 = q4 * 6, min(22, q4 * 6 + 6)
                        dma("pool", wdn[:, j0:j1, :], w_dn[j0 * 128:j1 * 128, :].rearrange("(k p) c -> p k c", p=128),
                            writes=[r_wdn])
                stage_ffn(tg)
            for i in range(4):
                stage_down(3, i)
            S.barrier()


    except _Stop:
        pass
    S.emit(nc, final_res=final_res + tap_res)
    return nc


def _consts():
    ident = np.eye(128, dtype=np.float32)
    rm = np.zeros((128, 128), np.float32)
    for m in range(128):
        if m % 64 < 32:
            rm[m + 32, m] = 1.0
        else:
            rm[m - 32, m] = 1.0
    k = np.arange(128)[:, None]
    q = np.arange(128)[None, :]
    diag = (q >= k).astype(np.float32)
    prev_a = (k > q).astype(np.float32)
    prev_b = (k >= q).astype(np.float32)
    ma = np.concatenate([diag, prev_a], axis=1)
    mb = np.concatenate([diag, prev_b], axis=1)
    mska = np.concatenate([ma, ma], axis=1)
    mskb = np.concatenate([mb, mb], axis=1)
    p = np.arange(128)
    invf = (np.float32(10000.0) ** (-(p % 32).astype(np.float32) / np.float32(32.0))).astype(np.float32)
    sgn = np.where(p % 64 < 32, -1.0, 1.0).astype(np.float32)
    vecs = np.stack([invf, sgn], axis=1).astype(np.float32)
    return dict(c_ident=ident, c_rm=rm, c_mska=mska, c_mskb=mskb, c_vecs=vecs)


_CACHE = {}


def _run(inputs, taps=(), stop=None):
    f = lambda a: np.ascontiguousarray(np.asarray(a))
    x = f(inputs["x"]); c = f(inputs["c"]); positions = f(inputs["positions"])
    shared = dict(
        w_ada=f(inputs["w_ada"][0]), b_ada=f(inputs["b_ada"]).reshape(1, -1), w_in=f(inputs["w_in"][0]),
        sinks=f(inputs["sinks"]).reshape(1, 16), w_a=f(inputs["w_branch_a"][0]), w_b=f(inputs["w_branch_b"][0]),
        w_o=f(inputs["w_o"][0]), ln1_g=f(inputs["ln1_g"]).reshape(1, -1), ln1_b=f(inputs["ln1_b"]).reshape(1, -1),
        ln2_g=f(inputs["ln2_g"]).reshape(1, -1), ln2_b=f(inputs["ln2_b"]).reshape(1, -1),
        w_gu=f(inputs["w_gate_up"][0]), w_dn=f(inputs["w_down"][0]))
    shared.update(_consts())
    key = (tuple(taps), stop)
    if key not in _CACHE:
        _CACHE[key] = build_program(taps, stop)
    nc = _CACHE[key]
    in_maps = []
    for i in range(NCORES):
        b0 = i * NSEQ
        m = dict(shared)
        m["x"] = f(x[b0:b0 + NSEQ].reshape(NSEQ * SEQ, D))
        m["cT"] = f(c[b0:b0 + NSEQ].T.reshape(8, 128, NSEQ).transpose(1, 0, 2))
        m["pos"] = f(positions[b0:b0 + NSEQ].astype(np.int32))
        in_maps.append(m)
    res = run_bass_kernel_spmd(nc, in_maps, core_ids=list(range(NCORES)))
    return res


def kernel(**inputs):
    res = _run(inputs)
    outs = [np.asarray(r["out"]).reshape(NSEQ, SEQ, D) for r in res.results]
    return np.concatenate(outs, axis=0).astype(np.float32)
```

```python
import math
import os
import numpy as np
LVL = int(os.environ.get('LVL', '9'))
import concourse.bass as bass
import concourse.mybir as mybir
from concourse.bass_utils import run_bass_kernel_spmd

F32 = mybir.dt.float32
BF16 = mybir.dt.bfloat16
I32 = mybir.dt.int32
ALU = mybir.AluOpType
AF = mybir.ActivationFunctionType

NCORES = 8
SEQ = 2048
D = 1024
NSEQ = 2
DFF = 2816
ALPHA = 2.0 ** 0.25
EPS = 1e-5
QA0, KA0, VA0, QB0, KB0, VB0, GA0, GB0 = 0, 1024, 1152, 1280, 2816, 4352, 5888, 6912

COMPUTE = ("pe", "act", "dve", "pool")
ENGS = ("pe", "act", "dve", "pool", "sp")


class Res:
    __slots__ = ("name", "last_w", "readers", "sem", "dma_n", "excl")

    def __init__(self, name, excl=False):
        self.name = name
        self.excl = excl
        self.last_w = None
        self.readers = []
        self.sem = None
        self.dma_n = 0


class Op:
    __slots__ = ("eng", "fn", "dma", "deps", "sig", "seq", "semres", "idx")


class Sched:
    def __init__(self):
        self.ops = []
        self.last_on = {e: None for e in ENGS}
        self.bar = {e: set() for e in ENGS}
        self.dma_since_bar = []

    def op(self, eng, fn, reads=(), writes=(), dma=False, semres=None):
        o = Op()
        o.eng, o.fn, o.dma, o.sig, o.seq = eng, fn, dma, False, 0
        o.idx = len(self.ops)
        raw, other = set(), set()
        for r in reads:
            if r.last_w is not None:
                raw.add(r.last_w)
            if r.excl:
                other.update(r.readers)
        for r in writes:
            if r.last_w is not None:
                other.add(r.last_w)
            other.update(r.readers)
        for r in reads:
            r.readers.append(o.idx)
        for r in writes:
            r.last_w = o.idx
            r.readers = []
        deps = set()
        for d in raw | other | self.bar[eng]:
            a = self.ops[d]
            if a.dma or dma or a.eng != eng:
                deps.add(d)
            elif eng != "pe" and d in raw:
                deps.add(d)
        self.bar[eng] = set()
        o.deps = sorted(deps)
        for d in o.deps:
            self.ops[d].sig = True
        if dma:
            if semres is None:
                semres = writes[0]
            o.semres = semres
            semres.dma_n += 1
            o.seq = semres.dma_n
            self.dma_since_bar.append(o.idx)
        else:
            o.semres = None
        self.ops.append(o)
        self.last_on[eng] = o.idx
        return o

    def barrier(self):
        s = set(self.dma_since_bar)
        for e in ENGS:
            if self.last_on[e] is not None:
                s.add(self.last_on[e])
        for e in ENGS:
            self.bar[e] = set(s) | self.bar[e]
        self.dma_since_bar = []

    def emit(self, nc, final_res=()):
        from contextlib import ExitStack
        cnt = {e: 0 for e in COMPUTE}
        for o in self.ops:
            if not o.dma and o.sig:
                cnt[o.eng] += 1
                o.seq = cnt[o.eng]
        with ExitStack() as es:
            esem = {e: es.enter_context(nc.semaphore("s_" + e)) for e in COMPUTE}
            nsem = 0
            for o in self.ops:
                if o.dma and o.semres.sem is None:
                    o.semres.sem = es.enter_context(nc.semaphore("d%d" % nsem))
                    nsem += 1
            self.nsem = nsem + 4
            block = es.enter_context(nc.Block())
            ops = self.ops

            def run(eng_name, eng):
                waited = {}
                for o in ops:
                    if o.eng != eng_name:
                        continue
                    need = {}
                    for d in o.deps:
                        a = ops[d]
                        if a.dma:
                            sem, val = a.semres.sem, 16 * a.seq
                        else:
                            sem, val = esem[a.eng], a.seq
                        k = id(sem)
                        if waited.get(k, 0) < val and need.get(k, (None, 0))[1] < val:
                            need[k] = (sem, val)
                    for k, (sem, val) in need.items():
                        eng.wait_ge(sem, val)
                        waited[k] = val
                    ins = o.fn(eng)
                    if o.dma:
                        ins.then_inc(o.semres.sem, 16)
                    elif o.sig:
                        ins.then_inc(esem[o.eng], 1)
                if eng_name == "sp":
                    for r in final_res:
                        if r.sem is not None and r.dma_n > 0:
                            eng.wait_ge(r.sem, 16 * r.dma_n)

            @block.tensor
            def _(e):
                run("pe", e)

            @block.scalar
            def _(e):
                run("act", e)

            @block.vector
            def _(e):
                run("dve", e)

            @block.gpsimd
            def _(e):
                run("pool", e)

            @block.sync
            def _(e):
                run("sp", e)


class Ring:
    def __init__(self, items):
        self.items = items
        self.i = 0

    def next(self):
        r = self.items[self.i % len(self.items)]
        self.i += 1
        return r


DT_BYTES = {F32: 4, BF16: 2, I32: 4}
KB = 1024


class _Stop(Exception):
    pass


def build_program(taps=(), stop=None):
    nc = bass.Bass("TRN2", target_bir_lowering=False)

    def ck(name):
        if stop == name:
            raise _Stop()
    S = Sched()
    uid = [0]

    def din(name, shape, dt=F32):
        return nc.dram_tensor(name, shape, dt, kind="ExternalInput").ap()

    x = din("x", [NSEQ * SEQ, D])
    cT = din("cT", [128, 8, NSEQ])
    pos = din("pos", [NSEQ, SEQ], I32)
    w_ada = din("w_ada", [D, 6 * D])
    b_ada = din("b_ada", [1, 6 * D])
    w_in = din("w_in", [D, 7936])
    sinks = din("sinks", [1, 16])
    w_a = din("w_a", [1024, D])
    w_b = din("w_b", [512, D])
    w_o = din("w_o", [D, D])
    ln_d = [din(n, [1, D]) for n in ("ln1_g", "ln1_b", "ln2_g", "ln2_b")]
    w_gu = din("w_gu", [D, 2 * DFF])
    w_dn = din("w_dn", [DFF, D])
    c_ident = din("c_ident", [128, 128])
    c_rm = din("c_rm", [128, 128])
    c_mska = din("c_mska", [128, 512])
    c_mskb = din("c_mskb", [128, 512])
    c_vecs = din("c_vecs", [128, 2])
    out = nc.dram_tensor("out", [NSEQ * SEQ, D], F32, kind="ExternalOutput").ap()
    tap_out = {}
    for (tn, shp) in taps:
        tap_out[tn] = nc.dram_tensor("tap_" + tn, list(shp), F32, kind="ExternalOutput").ap()

    SB_LO = 17408
    PERS = 28 * KB
    AR0 = SB_LO + PERS
    SB_TOP = 221184
    AR_SZ = SB_TOP - AR0

    class Arena:
        def __init__(self, lo, hi):
            self.lo, self.hi, self.p = lo, hi, lo

        def alloc(self, name, shape, dt):
            n = DT_BYTES[dt]
            for s_ in shape[1:]:
                n *= s_
            off = (self.p + 63) // 64 * 64
            self.p = off + n
            assert self.p <= self.hi, (name, self.p, self.hi)
            uid[0] += 1
            return nc.alloc_sbuf_tensor_at("%s_%d" % (name, uid[0]), list(shape), dt, offset=off)

    pers = Arena(SB_LO, SB_LO + PERS)
    ident = pers.alloc("ident", [128, 128], F32)
    rm = pers.alloc("rm", [128, 128], BF16)
    mska = pers.alloc("mska", [128, 2, 256], BF16)
    mskb = pers.alloc("mskb", [128, 2, 256], BF16)
    vecs = pers.alloc("vecs", [128, 2], F32)
    exps = pers.alloc("exps", [128, 16], F32)
    modT = pers.alloc("modT", [128, 4, 8, NSEQ], F32)
    cact = pers.alloc("cact", [128, 8, NSEQ], F32)
    ones_r = pers.alloc("ones_r", [1, 128], F32)
    sinkE = pers.alloc("sinkE", [128, 8], F32)
    sinkO = pers.alloc("sinkO", [128, 8], F32)
    bc = [pers.alloc("bc%d" % i, [128, D], F32) for i in range(6)]
    r_const = Res("const")
    r_modT = Res("modT")
    r_crep = Res("crep")
    r_bc = [Res("bc%d" % i) for i in range(6)]

    banks = []
    pairs = []
    for i in range(4):
        t = nc.alloc_psum_tensor("bpair%d" % i, [128, 1024], F32)
        rs = [Res("bank%d" % (2 * i + j), excl=True) for j in range(2)]
        pairs.append((t, rs))
        for j in range(2):
            banks.append((t[:, j * 512:(j + 1) * 512], rs[j]))

    def dma(q, out_ap, in_ap, reads=(), writes=(), semres=None):
        return S.op(q, lambda e: e.dma_start(out=out_ap, in_=in_ap), reads=reads, writes=writes,
                    dma=True, semres=semres)

    def mm(out_ap, lhsT, rhs, start, stop, reads, writes):
        return S.op("pe", lambda e: e.matmul(out_ap, lhsT=lhsT, rhs=rhs, start=start, stop=stop),
                    reads=reads, writes=writes)

    tap_res = []

    def tap(name, ap, reads):
        if name in tap_out:
            r = Res("tap_" + name)
            tap_res.append(r)
            dma("pool", tap_out[name], ap, reads=reads, semres=r)

    dma("sp", ident[:], c_ident, writes=[r_const])
    dma("sp", vecs[:], c_vecs, writes=[r_const])
    dma("pool", rm[:], c_rm, writes=[r_const])
    dma("pool", mska[:], c_mska.rearrange("p (h q) -> p h q", h=2), writes=[r_const])
    dma("pool", mskb[:], c_mskb.rearrange("p (h q) -> p h q", h=2), writes=[r_const])
    dma("sp", exps[:], sinks[0:1, :].partition_broadcast(128), writes=[r_const])
    dma("sp", cact[:], cT, writes=[r_const])
    for i in range(4):
        dma("sp", bc[i][:], ln_d[i][0:1, :].partition_broadcast(128), writes=[r_bc[i]])
    S.op("act", lambda e: e.activation(out=exps[:], in_=exps[:], func=AF.Exp), reads=[r_const], writes=[r_const])
    S.op("act", lambda e: e.activation(out=cact[:], in_=cact[:], func=AF.Silu), reads=[r_const], writes=[r_const])
    S.op("pool", lambda e: e.memset(ones_r[:], 1.0), writes=[r_const])
    S.op("dve", lambda e: e.memset(sinkE[:], 0.0), reads=[r_const], writes=[r_const])
    S.op("dve", lambda e: e.memset(sinkO[:], 0.0), reads=[r_const], writes=[r_const])
    S.op("dve", lambda e: e.tensor_copy(out=sinkE[64:128, :], in_=exps[:].rearrange("p (j t) -> p j t", t=2)[64:128, :, 0]),
         reads=[r_const], writes=[r_const])
    S.op("dve", lambda e: e.tensor_copy(out=sinkO[0:64, :], in_=exps[:].rearrange("p (j t) -> p j t", t=2)[0:64, :, 1]),
         reads=[r_const], writes=[r_const])
    S.barrier()
    gu_s = nc.dram_tensor("gu_s", [22, 128, 8 * 2 * 128], BF16, kind="Internal").ap()
    r_gus = Res("gus")
    for j in range(22):
        for g in range(2):
            dma("pool", gu_s[j].rearrange("p (k g c) -> p k g c", k=8, g=2)[:, :, g, :],
                w_gu[:, g * DFF + j * 128:g * DFF + (j + 1) * 128].rearrange("(k p) c -> p k c", p=128), writes=[r_gus])

    final_res = []

    def mod_columns():
        ar = Arena(AR0, AR0 + AR_SZ)
        wsl = [(ar.alloc("wada", [128, 8, 512], F32), Res("wada%d" % i)) for i in range(2)]
        bsl = [(ar.alloc("brow", [1, 512], F32), Res("brow%d" % i)) for i in range(2)]
        wr, br = Ring(wsl), Ring(bsl)
        pr = Ring(banks[0:2])
        for j, kind in enumerate((0, 1, 3, 4)):
            for half in range(2):
                c0 = kind * D + half * 512
                (wt, wres), (bt, bres) = wr.next(), br.next()
                dma("sp", wt[:], w_ada[:, c0:c0 + 512].rearrange("(k p) c -> p k c", p=128), writes=[wres])
                dma("sp", bt[:], b_ada[0:1, c0:c0 + 512], writes=[bres])
                for e4 in range(4):
                    (pt, pres) = pr.next()
                    for k in range(8):
                        mm(pt[:, 0:NSEQ], wt[:, k, e4 * 128:(e4 + 1) * 128], cact[:, k, :], k == 0, False,
                           [wres, r_const], [pres])
                    mm(pt[:, 0:NSEQ], bt[0:1, e4 * 128:(e4 + 1) * 128], ones_r[0:1, 0:NSEQ], False, True,
                       [bres, r_const], [pres])
                    kk = half * 4 + e4
                    addv = 1.0 if kind in (1, 4) else 0.0
                    S.op("dve", lambda e, pt=pt, j=j, kk=kk, addv=addv: e.tensor_scalar(
                        out=modT[:, j, kk, :], in0=pt[:, 0:NSEQ], scalar1=addv, scalar2=None, op0=ALU.add),
                        reads=[pres], writes=[r_modT])

    def mod_gate_rows(s):
        ar = Arena(AR0, AR0 + AR_SZ)
        wsl = [(ar.alloc("wadg", [128, 8, 512], F32), Res("wadg%d" % i)) for i in range(2)]
        bsl = [(ar.alloc("browg", [1, 512], F32), Res("browg%d" % i)) for i in range(2)]
        crep = ar.alloc("crep", [128, 8, 128], F32)
        wr, br = Ring(wsl), Ring(bsl)
        pr = Ring(banks[0:2])
        for k in range(8):
            S.op("dve", lambda e, k=k: e.tensor_scalar(out=crep[:, k, :], in0=ident[:], scalar1=0.0,
                                                       scalar2=cact[:, k, s:s + 1], op0=ALU.mult, op1=ALU.add),
                 reads=[r_const], writes=[r_crep])
        for gi, kind in enumerate((2, 5)):
            for half in range(2):
                c0 = kind * D + half * 512
                (wt, wres), (bt, bres) = wr.next(), br.next()
                dma("sp", wt[:], w_ada[:, c0:c0 + 512].rearrange("(k p) c -> p k c", p=128), writes=[wres])
                dma("sp", bt[:], b_ada[0:1, c0:c0 + 512], writes=[bres])
                (pt, pres) = pr.next()
                for k in range(8):
                    mm(pt[:], crep[:, k, :], wt[:, k, :], k == 0, False, [wres, r_crep], [pres])
                mm(pt[:], ones_r[0:1, :], bt[0:1, :], False, True, [bres, r_const], [pres])
                S.op("dve", lambda e, pt=pt, gi=gi, half=half: e.tensor_scalar(
                    out=bc[4 + gi][:, half * 512:(half + 1) * 512], in0=pt[:], scalar1=1.0, scalar2=None, op0=ALU.add),
                    reads=[pres], writes=[r_bc[4 + gi]])

    try:
        ck('const')
        mod_columns()
        S.barrier()
        ck('mod')

        for s in range(int(os.environ.get('NS', NSEQ))):
            mod_gate_rows(s)
            S.barrier()
            ck('gate')
            fx = Arena(AR0, AR0 + 96 * KB)
            uT = fx.alloc("uT", [128, 8, SEQ], BF16)
            cosT = fx.alloc("cosT", [128, SEQ], F32)
            sinS = fx.alloc("sinS", [128, SEQ], F32)
            oa_off = (fx.p + 63) // 64 * 64
            OA = fx.alloc("OA", [128, 8, SEQ], BF16)
            OB = fx.alloc("OB", [128, 4, SEQ], BF16)
            r_uT = [Res("uT%d" % t) for t in range(4)]
            r_cs = Res("cs")
            r_OA = [Res("OA%d" % j) for j in range(8)]
            r_OB = [Res("OB%d" % j) for j in range(4)]
            ub = Arena(AR0 + 96 * KB, AR0 + AR_SZ)
            wsl = Ring([(ub.alloc("wu", [128, 8, 3, 128], BF16), Res("wu%d" % i)) for i in range(2)])
            QTr = Ring([(ub.alloc("QT", [128, SEQ], BF16), Res("QT%d" % i)) for i in range(2)])
            KTr_items = [(ub.alloc("KT", [128, SEQ], BF16), Res("KT%d" % i)) for i in range(2)]
            KTr = Ring(KTr_items)
            VOr_items = [(ub.alloc("VO", [128, 16, 2, 128], BF16), Res("VO%d" % i)) for i in range(2)]
            VOr = Ring(VOr_items)
            acc_mark = ub.p
            accO = ub.alloc("accO", [128, SEQ], F32)
            accD = ub.alloc("accD", [128, SEQ], F32)
            r_accs = [Res("acc%d" % k) for k in range(4)]
            Pr = [(ub.alloc("P", [128, 2, 256], BF16), Res("P%d" % i)) for i in range(4)]
            e_off = (ub.p + 63) // 64 * 64
            Er = Ring([(ub.alloc("E", [128, 2, 256], BF16), Res("E%d" % i)) for i in range(2)])
            xbr = Ring([(ub.alloc("xb", [128, 512], BF16), Res("xb%d" % i)) for i in range(2)])
            t1r = Ring([(ub.alloc("t1", [128, 512], F32), Res("t1_%d" % i)) for i in range(2)])
            t2b = nc.alloc_sbuf_tensor_at("t2b_%d" % s, [128, 512], F32, offset=e_off)
            t2r = Ring([(ub.alloc("t2", [128, 512], F32), [Res("t2_0")]), (t2b, [Er.items[0][1], Er.items[1][1]])])
            tr = Arena(acc_mark, acc_mark + 16 * KB)
            posi = tr.alloc("posi", [128, SEQ], I32)
            xsr = Ring([(tr.alloc("xs", [128, D], F32), Res("xs%d" % i)) for i in range(2)])
            r_posi = Res("posi")
            dma("sp", posi[:], pos[s:s + 1, :].partition_broadcast(128), writes=[r_posi])
            tmpa = nc.alloc_sbuf_tensor_at("tmpa_%d" % s, [128, SEQ], F32, offset=oa_off)
            I2P = 1.0 / (2 * math.pi)
            rr = [r_posi, r_cs]
            S.op("dve", lambda e: e.tensor_copy(out=cosT[:], in_=posi[:]), reads=rr, writes=rr)
            S.op("dve", lambda e: e.tensor_scalar(out=cosT[:], in0=cosT[:], scalar1=vecs[:, 0:1], scalar2=None, op0=ALU.mult),
                 reads=rr + [r_const], writes=rr)
            S.op("dve", lambda e: e.tensor_scalar(out=sinS[:], in0=cosT[:], scalar1=I2P, scalar2=None, op0=ALU.mult), reads=rr, writes=rr)
            S.op("dve", lambda e: e.tensor_copy(out=posi[:], in_=sinS[:]), reads=rr, writes=rr)
            S.op("dve", lambda e: e.tensor_copy(out=sinS[:], in_=posi[:]), reads=rr, writes=rr)
            S.op("dve", lambda e: e.scalar_tensor_tensor(out=sinS[:], in0=sinS[:], scalar=-2 * math.pi, in1=cosT[:],
                                                         op0=ALU.mult, op1=ALU.add), reads=rr, writes=rr)
            S.op("dve", lambda e: e.tensor_scalar(out=sinS[:], in0=sinS[:], scalar1=math.pi, scalar2=-math.pi,
                                                  op0=ALU.min, op1=ALU.max), reads=rr, writes=rr)
            S.op("dve", lambda e: e.tensor_scalar(out=tmpa[:], in0=cosT[:], scalar1=0.5 * math.pi, scalar2=None, op0=ALU.add),
                 reads=rr, writes=rr)
            S.op("dve", lambda e: e.tensor_scalar(out=cosT[:], in0=tmpa[:], scalar1=I2P, scalar2=None, op0=ALU.mult), reads=rr, writes=rr)
            S.op("dve", lambda e: e.tensor_copy(out=posi[:], in_=cosT[:]), reads=rr, writes=rr)
            S.op("dve", lambda e: e.tensor_copy(out=cosT[:], in_=posi[:]), reads=rr, writes=rr)
            S.op("dve", lambda e: e.scalar_tensor_tensor(out=cosT[:], in0=cosT[:], scalar=-2 * math.pi, in1=tmpa[:],
                                                         op0=ALU.mult, op1=ALU.add), reads=rr, writes=rr)
            S.op("dve", lambda e: e.tensor_scalar(out=cosT[:], in0=cosT[:], scalar1=math.pi, scalar2=-math.pi,
                                                  op0=ALU.min, op1=ALU.max), reads=rr, writes=rr)
            S.op("act", lambda e: e.activation(out=sinS[:], in_=sinS[:], func=AF.Sin), reads=rr, writes=rr)
            S.op("act", lambda e: e.activation(out=cosT[:], in_=cosT[:], func=AF.Sin), reads=rr, writes=rr)
            S.op("dve", lambda e: e.tensor_scalar(out=sinS[:], in0=sinS[:], scalar1=vecs[:, 1:2], scalar2=None, op0=ALU.mult),
                 reads=rr + [r_const], writes=rr)
            for i in range(16):
                (xt, xres) = xsr.next()
                dma("sp", xt[:], x[s * SEQ + i * 128: s * SEQ + (i + 1) * 128, :], writes=[xres])
                for hf in range(2):
                    (pt, pres) = banks[hf + 2 * (i % 2)]
                    for k4 in range(4):
                        k = hf * 4 + k4
                        S.op("pe", lambda e, pt=pt, k4=k4, xt=xt, k=k: e.transpose(
                            pt[:, k4 * 128:(k4 + 1) * 128], xt[:, k * 128:(k + 1) * 128], ident[:]),
                            reads=[xres, r_const], writes=[pres])
                    for k4 in range(4):
                        k = hf * 4 + k4
                        S.op("act", lambda e, pt=pt, k4=k4, k=k, i=i, s=s: e.activation(
                            out=uT[:, k, i * 128:(i + 1) * 128], in_=pt[:, k4 * 128:(k4 + 1) * 128], func=AF.Identity,
                            scale=modT[:, 1, k, s:s + 1], bias=modT[:, 0, k, s:s + 1]),
                            reads=[pres, r_modT], writes=[r_uT[i // 4]])
            S.barrier()
            tap("uT%d" % s, uT[:, 0, :], r_uT)
            tap("cos%d" % s, cosT[:], [r_cs])
            tap("sin%d" % s, sinS[:], [r_cs])
            ck('ut')

            proj_banks = Ring(banks[0:2])
            rot_banks = Ring(banks[2:4])
            s_pairs = Ring(pairs[2:4])
            pv_banks = Ring(banks[2:4])

            def load_w(cols):
                (wt, wres) = wsl.next()
                for pi, c0 in enumerate(cols):
                    dma("pool", wt[:, :, pi, :], w_in[:, c0:c0 + 128].rearrange("(k p) c -> p k c", p=128), writes=[wres])
                return wt, wres

            def proj_rope(wt, wres, part, dest, dres, r, between=None):
                def stage_b(st):
                    (tg, xb, xbres, t1, t1res) = st
                    (t2, t2res) = t2r.next()
                    (rt, rres) = rot_banks.next()
                    mm(rt[:], rm[:], xb[:], True, True, [xbres, r_const], [rres])
                    S.op("dve", lambda e, t2=t2, rt=rt, tg=tg: e.tensor_tensor(
                        out=t2[:], in0=rt[:], in1=sinS[:, tg * 512:(tg + 1) * 512], op=ALU.mult),
                        reads=[rres, r_cs], writes=t2res)
                    n = 512 // r
                    dv = dest[:].rearrange("p (c i) -> p c i", c=r)[:, :, tg * n:(tg + 1) * n]
                    a1 = t1[:].rearrange("p (i c) -> p c i", c=r)
                    a2 = t2[:].rearrange("p (i c) -> p c i", c=r)
                    S.op("pool", lambda e, dv=dv, a1=a1, a2=a2: e.tensor_tensor(out=dv, in0=a1, in1=a2, op=ALU.add),
                         reads=[t1res] + t2res, writes=[dres])

                pend = None
                for tg in range(4):
                    (pt, pres) = proj_banks.next()
                    for k in range(8):
                        mm(pt[:], wt[:, k, part, :], uT[:, k, tg * 512:(tg + 1) * 512], k == 0, k == 7,
                           [wres, r_uT[tg]], [pres])
                    (xb, xbres) = xbr.next()
                    (t1, t1res) = t1r.next()
                    S.op("act", lambda e, xb=xb, pt=pt: e.activation(out=xb[:], in_=pt[:], func=AF.Copy),
                         reads=[pres], writes=[xbres])
                    S.op("dve", lambda e, t1=t1, pt=pt, tg=tg: e.tensor_tensor(
                        out=t1[:], in0=pt[:], in1=cosT[:, tg * 512:(tg + 1) * 512], op=ALU.mult),
                        reads=[pres, r_cs], writes=[t1res])
                    if pend is not None:
                        stage_b(pend)
                    pend = (tg, xb, xbres, t1, t1res)
                    if between:
                        between.pop(0)()
                stage_b(pend)
                while between:
                    between.pop(0)()

            def proj_v(wt, wres, part, vo, vores, r, ncols_layout):
                nblk = 16 // r
                for c in range(r):
                    for kb in range(nblk):
                        (pt, pres) = proj_banks.next()
                        for k in range(8):
                            lt = uT[:, k, :].rearrange("p (i c) -> p c i", c=r)[:, c, kb * 128:(kb + 1) * 128]
                            mm(pt[:, 0:128], lt, wt[:, k, part, :], k == 0, k == 7, [wres] + r_uT, [pres])
                        for (vt, vr, cmap) in ncols_layout:
                            for (h, d0, s0) in cmap:
                                S.op("act", lambda e, vt=vt, pt=pt, h=h, d0=d0, s0=s0, c=c, kb=kb: e.activation(
                                    out=vt[:, c * nblk + kb, h, d0:d0 + 64], in_=pt[:, s0:s0 + 64], func=AF.Copy),
                                    reads=[pres], writes=[vr])

            def attention(qt, qres, kt, kres, vo, vores, r, msk, first, tog, sinkj=None, fin_cb=None):
                nblk = 16 // r
                clen = SEQ // r
                nT = (nblk + 1) // 2
                steps = [(c, m) for c in range(r) for m in range(nT)]

                def emit_S(c, m):
                    lst = []
                    for kb in range(2 * m, min(2 * m + 2, nblk)):
                        nq = 256 if kb + 1 < nblk else 128
                        (st, sresl) = s_pairs.next()
                        sv = st[:].rearrange("p (h q) -> p h q", h=2)
                        for h in range(2):
                            mm(sv[:, h, 0:nq], kt[64 * h:64 * h + 64, c * clen + kb * 128: c * clen + (kb + 1) * 128],
                               qt[64 * h:64 * h + 64, c * clen + kb * 128: c * clen + kb * 128 + nq], True, True,
                               [kres, qres], [sresl[h]])
                        lst.append((kb, nq, sv, sresl))
                    return lst

                def emit_exp(c, item):
                    (kb, nq, sv, sresl) = item
                    (et, eres) = Er.next()
                    S.op("act", lambda e, et=et, sv=sv, nq=nq: e.activation(
                        out=et[:, :, 0:nq], in_=sv[:, :, 0:nq], func=AF.Exp, scale=0.125),
                        reads=sresl, writes=[eres])
                    return (et, eres)

                def emit_mask(c, item, e_, eng):
                    (kb, nq, sv, sresl) = item
                    (et, eres) = e_
                    (pt_, pres_) = Pr[(c * nblk + kb) % 4]
                    S.op(eng, lambda e, pt_=pt_, et=et, nq=nq: e.tensor_tensor(
                        out=pt_[:, :, 0:nq], in0=et[:, :, 0:nq], in1=msk[:, :, 0:nq], op=ALU.mult),
                        reads=[eres, r_const], writes=[pres_])

                def emit_PV(c, m):
                    W = min(256, nblk * 128)
                    (pv, pvres) = pv_banks.next()
                    pvv = pv[:].rearrange("p (h q) -> p h q", h=2)
                    g0 = c * nblk
                    for h in range(2):
                        has2 = (2 * m - 1 >= 0)
                        has3 = (2 * m + 1 < nblk)
                        P0 = Pr[(g0 + 2 * m) % 4]
                        mm(pvv[:, h, 0:W], vo[:, g0 + 2 * m, h, :], P0[0][:, h, 0:W], True,
                           not (has2 or has3), [vores, P0[1]], [pvres])
                        if has2:
                            P2 = Pr[(g0 + 2 * m - 1) % 4]
                            mm(pvv[:, h, 0:128], vo[:, g0 + 2 * m - 1, h, :], P2[0][:, h, 128:256],
                               False, not has3, [vores, P2[1]], [pvres])
                        if has3:
                            P3 = Pr[(g0 + 2 * m + 1) % 4]
                            mm(pvv[:, h, 128:256], vo[:, g0 + 2 * m + 1, h, :], P3[0][:, h, 0:128],
                               False, True, [vores, P3[1]], [pvres])

                    def nat(t_, lo, hi):
                        return t_[lo:hi, :].rearrange("p (i c) -> p c i", c=r)[:, c, 256 * m:256 * m + W]
                    prs = [(nat(accO, 0, 128), pvv[:, 0, 0:W]), (nat(accD, 0, 128), pvv[:, 1, 0:W])]

                    lo = r * 256 * m + c
                    hi = lo + r * (W - 1)
                    rch = [r_accs[k] for k in range(lo // 512, hi // 512 + 1)]

                    def do_acc():
                        for hi_, (dst, src) in enumerate(prs):
                            if first and sinkj is not None:
                                sc = (sinkE if hi_ == 0 else sinkO)[:, sinkj:sinkj + 1]
                                S.op("dve", lambda e, dst=dst, src=src, sc=sc: e.tensor_scalar(
                                    out=dst, in0=src, scalar1=sc, scalar2=None, op0=ALU.add),
                                    reads=[pvres, r_const], writes=rch)
                            elif first:
                                S.op("dve", lambda e, dst=dst, src=src: e.tensor_copy(out=dst, in_=src),
                                     reads=[pvres], writes=rch)
                            else:
                                S.op("dve", lambda e, dst=dst, src=src: e.tensor_tensor(out=dst, in0=src, in1=dst, op=ALU.add),
                                     reads=[pvres] + rch, writes=rch)
                    return do_acc

                pend = None
                for t in range(len(steps) + 1):
                    cur = None
                    late = None
                    if t < len(steps):
                        c, m = steps[t]
                        lst = emit_S(c, m)
                        cur = (c, m)
                        e0 = emit_exp(c, lst[0])
                        emit_mask(c, lst[0], e0, "pool")
                        if len(lst) > 1:
                            e1 = emit_exp(c, lst[1])
                            late = (c, lst[1], e1)
                    acc_fn = None
                    if pend is not None:
                        acc_fn = emit_PV(pend[0], pend[1])
                    if late is not None:
                        emit_mask(late[0], late[1], late[2], "dve")
                    if acc_fn is not None:
                        acc_fn()
                        if fin_cb is not None and pend[1] % 2 == 1:
                            fin_cb(pend[1] // 2)
                    pend = cur

            fin_ring = Ring([(t_, [r_]) for (t_, r_) in t1r.items] + list(t2r.items))

            def fin_stage1(ch, ring):
                cs_ = slice(ch * 512, (ch + 1) * 512)
                ra = [r_accs[ch]]
                (tmp, tres) = ring.next()
                S.op("dve", lambda e, tmp=tmp, cs_=cs_: e.tensor_copy(out=tmp[0:64, :], in_=accO[64:128, cs_]),
                     reads=ra, writes=tres)
                S.op("dve", lambda e, tmp=tmp, cs_=cs_: e.tensor_copy(out=tmp[64:128, :], in_=accD[0:64, cs_]),
                     reads=ra, writes=tres)
                S.op("act", lambda e, tmp=tmp: e.activation(out=tmp[:], in_=tmp[:], func=AF.Ln), reads=tres, writes=tres)
                S.op("act", lambda e, tmp=tmp: e.activation(out=tmp[:], in_=tmp[:], func=AF.Exp, scale=-1.0), reads=tres, writes=tres)
                return (tmp, tres)

            def fin_stage2(dest, dres, ch, tt, mul_eng):
                (tmp, tres) = tt
                cs_ = slice(ch * 512, (ch + 1) * 512)
                ra = [r_accs[ch]]
                S.op(mul_eng, lambda e, tmp=tmp, cs_=cs_: e.tensor_tensor(
                    out=dest[0:64, cs_], in0=accO[0:64, cs_], in1=tmp[0:64, :], op=ALU.mult),
                    reads=ra + tres, writes=[dres])
                S.op(mul_eng, lambda e, tmp=tmp, cs_=cs_: e.tensor_tensor(
                    out=dest[64:128, cs_], in0=accD[64:128, cs_], in1=tmp[64:128, :], op=ALU.mult),
                    reads=ra + tres, writes=[dres])

            fin_ring = Ring([(t_, [r_]) for (t_, r_) in t1r.items] + list(t2r.items))

            def finalize(dest, dres, sink_heads, mul_eng):
                tts = [fin_stage1(ch, fin_ring) for ch in range(4)]
                for ch in range(4):
                    fin_stage2(dest, dres, ch, tts[ch], mul_eng)

            tog = [0]
            for (vt, vr) in VOr_items:
                S.op("pool", lambda e, vt=vt: e.memset(vt[:], 1.0), writes=[vr])
            b_units = [(p, g, r) for p in range(4) for g, r in enumerate((1, 4, 16))]

            def bcols(p, g):
                hc = (g * 8 + 2 * p) * 64
                return [QB0 + hc, KB0 + hc, VB0 + hc]
            w_next = load_w(bcols(0, 0))
            pend_fin = [None]

            def flush_fin():
                if pend_fin[0] is not None:
                    finalize(*pend_fin[0])
                    pend_fin[0] = None
            for ui, (p, g, r) in enumerate(b_units):
                wt, wres = w_next
                (qt, qres) = QTr.next()
                (kt, kres) = KTr.next()
                (vo, vores) = VOr.next()
                proj_rope(wt, wres, 0, qt, qres, r)
                proj_rope(wt, wres, 1, kt, kres, r)
                ck('b1')
                proj_v(wt, wres, 2, vo, vores, r, [(vo, vores, [(0, 0, 0), (1, 64, 64)])])
                ck('b1v')
                if ui + 1 < len(b_units):
                    w_next = load_w(bcols(b_units[ui + 1][0], b_units[ui + 1][1]))
                else:
                    w_next = load_w([KA0, VA0])
                flush_fin()
                attention(qt, qres, kt, kres, vo, vores, r, mskb, g == 0, tog)
                ck('b1a')
                if g == 2:
                    pend_fin[0] = (OB[:, p, :], r_OB[p], None, "pool")
                    ck('bfin')
            tap("OB%d" % s, OB[:, 0, :], r_OB)
            ck('B')
            KAd = KTr_items
            VOA = VOr_items
            wt, wres = w_next
            w_next = load_w([QA0])
            (KAn, r_KAn) = QTr.next()
            proj_rope(wt, wres, 0, KAn, r_KAn, 1)
            for kv in range(2):
                (kd, kdres) = KAd[kv]
                for h in range(2):
                    S.op("dve", lambda e, kd=kd, kv=kv, h=h: e.tensor_copy(
                        out=kd[64 * h:64 * h + 64, :], in_=KAn[64 * kv:64 * kv + 64, :]),
                        reads=[r_KAn], writes=[kdres])
            proj_v(wt, wres, 1, None, None, 1,
                   [(VOA[0][0], VOA[0][1], [(0, 0, 0), (1, 64, 0)]), (VOA[1][0], VOA[1][1], [(0, 0, 64), (1, 64, 64)])])
            att_ring = Ring([(t_, [r_]) for (t_, r_) in t1r.items] + [t2r.items[0]])
            for j in range(8):
                wt, wres = w_next
                (qt, qres) = QTr.next()
                proj_rope(wt, wres, 0, qt, qres, 1)
                if j + 1 < 8:
                    w_next = load_w([QA0 + (j + 1) * 128])
                kv = j // 4
                flush_fin()
                st = [None]

                def fcb(ch, j=j, st=st):
                    tt = fin_stage1(ch, att_ring)
                    if st[0] is not None:
                        fin_stage2(OA[:, j, :], r_OA[j], st[0][0], st[0][1], "dve")
                    st[0] = (ch, tt)
                attention(qt, qres, KAd[kv][0], KAd[kv][1], VOA[kv][0], VOA[kv][1], 1, mska, True, tog, sinkj=j, fin_cb=fcb)
                fin_stage2(OA[:, j, :], r_OA[j], st[0][0], st[0][1], "dve")
            S.barrier()
            tap("OA%d" % s, OA[:, 0, :], r_OA)
            ck('A')

            mg = nc.alloc_sbuf_tensor_at("merged_%d" % s, [128, 8, SEQ], BF16, offset=AR0 + 96 * KB)
            r_mg = [Res("mg%d" % t) for t in range(4)]
            wo_t = nc.alloc_sbuf_tensor_at("wo_%d" % s, [128, 8, D], BF16, offset=AR0 + 128 * KB)
            r_wo = Res("wo")
            pm = Arena(AR0 + 144 * KB, AR0 + AR_SZ)
            wmr = Ring([(pm.alloc("wm", [128, 28, 128], BF16), Res("wm%d" % i)) for i in range(2)])
            sar = Ring([(pm.alloc("sa", [128, 512], F32), Res("sa%d" % i)) for i in range(2)])
            sbr = Ring([(pm.alloc("sb", [128, 512], F32), Res("sb%d" % i)) for i in range(2)])
            for hf in range(2):
                dma("pool", wo_t[:, :, hf * 512:(hf + 1) * 512],
                    w_o[:, hf * 512:(hf + 1) * 512].rearrange("(k p) c -> p k c", p=128), writes=[r_wo])
            bring = Ring([banks[0:4], banks[4:8]])
            def load_wm(dch):
                (wm, wmres) = wmr.next()
                cs_ = slice(dch * 128, (dch + 1) * 128)
                dma("pool", wm[:, 0:8, :], w_a[:, cs_].rearrange("(k p) c -> p k c", p=128), writes=[wmres])
                dma("pool", wm[:, 8:12, :], w_b[:, cs_].rearrange("(k p) c -> p k c", p=128), writes=[wmres])
                dma("pool", wm[:, 12:20, :], w_in[:, GA0 + dch * 128:GA0 + (dch + 1) * 128].rearrange("(k p) c -> p k c", p=128),
                    writes=[wmres])
                dma("pool", wm[:, 20:28, :], w_in[:, GB0 + dch * 128:GB0 + (dch + 1) * 128].rearrange("(k p) c -> p k c", p=128),
                    writes=[wmres])
                return wm, wmres
            wm_next = load_wm(0)
            for dch in range(8):
                (wm, wmres) = wm_next
                if dch + 1 < 8:
                    wm_next = load_wm(dch + 1)
                for tg in range(4):
                    bk = bring.next()
                    ts_ = slice(tg * 512, (tg + 1) * 512)
                    (pya, rya), (pyb, ryb), (pga, rga), (pgb, rgb) = bk
                    for k in range(8):
                        mm(pya[:], wm[:, k, :], OA[:, k, ts_], k == 0, k == 7, [wmres, r_OA[k]], [rya])
                    for k in range(4):
                        mm(pyb[:], wm[:, 8 + k, :], OB[:, k, ts_], k == 0, k == 3, [wmres, r_OB[k]], [ryb])
                    for k in range(8):
                        mm(pga[:], wm[:, 12 + k, :], uT[:, k, ts_], k == 0, k == 7, [wmres, r_uT[tg]], [rga])
                    for k in range(8):
                        mm(pgb[:], wm[:, 20 + k, :], uT[:, k, ts_], k == 0, k == 7, [wmres, r_uT[tg]], [rgb])
                    (sa, sares) = sar.next()
                    (sb_, sbres) = sbr.next()
                    S.op("act", lambda e, sa=sa, pga=pga: e.activation(out=sa[:], in_=pga[:], func=AF.Sigmoid),
                         reads=[rga], writes=[sares])
                    S.op("act", lambda e, sb_=sb_, pgb=pgb: e.activation(out=sb_[:], in_=pgb[:], func=AF.Sigmoid),
                         reads=[rgb], writes=[sbres])
                    S.op("dve", lambda e, sa=sa, pya=pya: e.tensor_tensor(out=sa[:], in0=pya[:], in1=sa[:], op=ALU.mult),
                         reads=[rya, sares], writes=[sares])
                    S.op("dve", lambda e, sb_=sb_, pyb=pyb: e.tensor_tensor(out=sb_[:], in0=pyb[:], in1=sb_[:], op=ALU.mult),
                         reads=[ryb, sbres], writes=[sbres])
                    S.op("pool", lambda e, sa=sa, sb_=sb_, dch=dch, ts_=ts_: e.tensor_tensor(
                        out=mg[:, dch, ts_], in0=sa[:], in1=sb_[:], op=ALU.add),
                        reads=[sares, sbres], writes=[r_mg[tg]])
            S.barrier()
            tap("mg%d" % s, mg[:, 0, :], r_mg)
            ck('PM')

            pf = Arena(AR0, AR0 + 96 * KB)
            pf2 = Arena(AR0 + 144 * KB, AR0 + AR_SZ)
            wdn = pf.alloc("wdn", [128, 22, D], BF16)
            r_wdn = Res("wdn")
            act_t = pf.alloc("actT", [128, 22, 512], BF16)
            r_act = [Res("act%d" % j) for j in range(22)]
            x1 = pf.alloc("x1", [128, 4, D], F32)
            r_x1 = [Res("x1_%d" % i) for i in range(4)]
            u2T = pf.alloc("u2T", [128, 8, 512], BF16)
            r_u2 = Res("u2T")
            xrr = Ring([(pf2.alloc("xr", [128, D], F32), Res("xr%d" % i)) for i in range(2)])
            yor = Ring([(pf2.alloc("yo", [128, D], F32), Res("yo%d" % i)) for i in range(2)])
            silr = Ring([(pf2.alloc("sil", [128, 512], F32), Res("sil%d" % i)) for i in range(1)])
            gur = Ring([((pf if i == 2 else pf2).alloc("gu", [128, 8, 2, 128], BF16), Res("gu%d" % i)) for i in range(3)])
            stat = pf2.alloc("stat", [128, 12], F32)
            mv = pf2.alloc("mv", [128, 4], F32)
            r_stat = Res("stat")
            final_res.extend([r for (_, r) in yor.items])

            def layernorm(zt, zres, gi, bi, dst, dres):
                for cch in range(2):
                    S.op("dve", lambda e, cch=cch: e.bn_stats(out=stat[:, cch * 6:(cch + 1) * 6], in_=zt[:, cch * 512:(cch + 1) * 512]),
                         reads=[zres], writes=[r_stat])
                S.op("dve", lambda e: e.bn_aggr(out=mv[:, 0:2], in_=stat[:]), reads=[r_stat], writes=[r_stat])
                S.op("act", lambda e: e.activation(out=mv[:, 2:3], in_=mv[:, 1:2], func=AF.Sqrt, bias=EPS, scale=1.0),
                     reads=[r_stat], writes=[r_stat])
                S.op("dve", lambda e: e.reciprocal(out=mv[:, 3:4], in_=mv[:, 2:3]), reads=[r_stat], writes=[r_stat])
                S.op("dve", lambda e: e.tensor_scalar(out=zt[:], in0=zt[:], scalar1=mv[:, 0:1], scalar2=mv[:, 3:4],
                                                      op0=ALU.subtract, op1=ALU.mult), reads=[zres, r_stat], writes=[zres])
                S.op("pool", lambda e: e.tensor_tensor(out=zt[:], in0=zt[:], in1=bc[gi][:], op=ALU.mult),
                     reads=[zres, r_bc[gi]], writes=[zres])
                S.op("pool", lambda e: e.tensor_tensor(out=dst, in0=zt[:], in1=bc[bi][:], op=ALU.add),
                     reads=[zres, r_bc[bi]], writes=[dres])

            ybanks = Ring([banks[0:2], banks[2:4]])
            hbanks = Ring([banks[4:6], banks[6:8]])
            def stage_wo(tg, i):
                row0 = s * SEQ + tg * 512 + i * 128
                (xr, xrres) = xrr.next()
                dma("sp", xr[:], x[row0:row0 + 128, :], writes=[xrres])
                yb_ = ybanks.next()
                for hf in range(2):
                    (py, ry) = yb_[hf]
                    for k in range(8):
                        mm(py[:], mg[:, k, tg * 512 + i * 128: tg * 512 + (i + 1) * 128], wo_t[:, k, hf * 512:(hf + 1) * 512],
                           k == 0, k == 7, [r_mg[tg], r_wo], [ry])
                zt = x1[:, i, :]
                for hf in range(2):
                    (py, ry) = yb_[hf]
                    S.op("dve", lambda e, py=py, hf=hf, i=i: e.tensor_tensor(
                        out=x1[:, i, hf * 512:(hf + 1) * 512], in0=py[:], in1=bc[4][:, hf * 512:(hf + 1) * 512], op=ALU.mult),
                        reads=[ry, r_bc[4]], writes=[r_x1[i]])
                S.op("dve", lambda e, i=i, xr=xr: e.scalar_tensor_tensor(
                    out=x1[:, i, :], in0=xr[:], scalar=ALPHA, in1=x1[:, i, :], op0=ALU.mult, op1=ALU.add),
                    reads=[xrres, r_x1[i]], writes=[r_x1[i]])
                layernorm(zt, r_x1[i], 0, 1, zt, r_x1[i])

            def stage_tr(tg, i):
                hb = hbanks.next()
                for hf in range(2):
                    (pt, pres) = hb[hf]
                    for k4 in range(4):
                        k = hf * 4 + k4
                        S.op("pe", lambda e, pt=pt, k4=k4, k=k, i=i: e.transpose(
                            pt[:, k4 * 128:(k4 + 1) * 128], x1[:, i, k * 128:(k + 1) * 128], ident[:]),
                            reads=[r_x1[i], r_const], writes=[pres])
                    for k4 in range(4):
                        k = hf * 4 + k4
                        S.op("act", lambda e, pt=pt, k4=k4, k=k, i=i, s=s: e.activation(
                            out=u2T[:, k, i * 128:(i + 1) * 128], in_=pt[:, k4 * 128:(k4 + 1) * 128], func=AF.Identity,
                            scale=modT[:, 3, k, s:s + 1], bias=modT[:, 2, k, s:s + 1]),
                            reads=[pres, r_modT], writes=[r_u2])

            def stage_ffn(tg):
                for j in range(22):
                    (gw, gwres) = gur.next()
                    dma("sp", gw[:], gu_s[j].rearrange("p (k g c) -> p k g c", k=8, g=2), reads=[r_gus], writes=[gwres])
                    hb = hbanks.next()
                    (pg, rg), (pu, ru) = hb
                    for k in range(8):
                        mm(pg[:], gw[:, k, 0, :], u2T[:, k, :], k == 0, k == 7, [gwres, r_u2], [rg])
                    for k in range(8):
                        mm(pu[:], gw[:, k, 1, :], u2T[:, k, :], k == 0, k == 7, [gwres, r_u2], [ru])
                    (sl, slres) = silr.next()
                    S.op("act", lambda e, sl=sl, pg=pg: e.activation(out=sl[:], in_=pg[:], func=AF.Silu),
                         reads=[rg], writes=[slres])
                    S.op("dve", lambda e, sl=sl, pu=pu, j=j: e.tensor_tensor(out=act_t[:, j, :], in0=pu[:], in1=sl[:], op=ALU.mult),
                         reads=[ru, slres], writes=[r_act[j]])

            def stage_down(tg, i):
                row0 = s * SEQ + tg * 512 + i * 128
                yb_ = ybanks.next()
                for hf in range(2):
                    (py, ry) = yb_[hf]
                    for j in range(22):
                        mm(py[:], act_t[:, j, i * 128:(i + 1) * 128], wdn[:, j, hf * 512:(hf + 1) * 512],
                           j == 0, j == 21, [r_act[j], r_wdn], [ry])
                (yo, yores) = yor.next()
                for hf in range(2):
                    (py, ry) = yb_[hf]
                    S.op("dve", lambda e, py=py, hf=hf, yo=yo: e.tensor_tensor(
                        out=yo[:, hf * 512:(hf + 1) * 512], in0=py[:], in1=bc[5][:, hf * 512:(hf + 1) * 512], op=ALU.mult),
                        reads=[ry, r_bc[5]], writes=[yores])
                S.op("dve", lambda e, i=i, yo=yo: e.scalar_tensor_tensor(
                    out=yo[:], in0=x1[:, i, :], scalar=ALPHA, in1=yo[:], op0=ALU.mult, op1=ALU.add),
                    reads=[r_x1[i], yores], writes=[yores])
                layernorm(yo, yores, 2, 3, yo[:], yores)
                dma("sp", out[row0:row0 + 128, :], yo[:], reads=[yores], semres=yores)

            for tg in range(4):
                for i in range(4):
                    if tg > 0:
                        stage_down(tg - 1, i)
                    stage_wo(tg, i)
                    if i > 0:
                        stage_tr(tg, i - 1)
                stage_tr(tg, 3)
                if tg == 0:
                    tap("x1_%d" % s, x1[:, 0, :], r_x1)
                    for q4 in range(4):
                        j0, j1 = q4 * 6, min(22, q4 * 6 + 6)
                        dma("pool", wdn[:, j0:j1, :], w_dn[j0 * 128:j1 * 128, :].rearrange("(k p) c -> p k c", p=128),
                            writes=[r_wdn])
                stage_ffn(tg)
            for i in range(4):
                stage_down(3, i)
            S.barrier()


    except _Stop:
        pass
    S.emit(nc, final_res=final_res + tap_res)
    return nc


def _consts():
    ident = np.eye(128, dtype=np.float32)
    rm = np.zeros((128, 128), np.float32)
    for m in range(128):
        if m % 64 < 32:
            rm[m + 32, m] = 1.0
        else:
            rm[m - 32, m] = 1.0
    k = np.arange(128)[:, None]
    q = np.arange(128)[None, :]
    diag = (q >= k).astype(np.float32)
    prev_a = (k > q).astype(np.float32)
    prev_b = (k >= q).astype(np.float32)
    ma = np.concatenate([diag, prev_a], axis=1)
    mb = np.concatenate([diag, prev_b], axis=1)
    mska = np.concatenate([ma, ma], axis=1)
    mskb = np.concatenate([mb, mb], axis=1)
    p = np.arange(128)
    invf = (np.float32(10000.0) ** (-(p % 32).astype(np.float32) / np.float32(32.0))).astype(np.float32)
    sgn = np.where(p % 64 < 32, -1.0, 1.0).astype(np.float32)
    vecs = np.stack([invf, sgn], axis=1).astype(np.float32)
    return dict(c_ident=ident, c_rm=rm, c_mska=mska, c_mskb=mskb, c_vecs=vecs)


_CACHE = {}


def _run(inputs, taps=(), stop=None):
    f = lambda a: np.ascontiguousarray(np.asarray(a))
    x = f(inputs["x"]); c = f(inputs["c"]); positions = f(inputs["positions"])
    shared = dict(
        w_ada=f(inputs["w_ada"][0]), b_ada=f(inputs["b_ada"]).reshape(1, -1), w_in=f(inputs["w_in"][0]),
        sinks=f(inputs["sinks"]).reshape(1, 16), w_a=f(inputs["w_branch_a"][0]), w_b=f(inputs["w_branch_b"][0]),
        w_o=f(inputs["w_o"][0]), ln1_g=f(inputs["ln1_g"]).reshape(1, -1), ln1_b=f(inputs["ln1_b"]).reshape(1, -1),
        ln2_g=f(inputs["ln2_g"]).reshape(1, -1), ln2_b=f(inputs["ln2_b"]).reshape(1, -1),
        w_gu=f(inputs["w_gate_up"][0]), w_dn=f(inputs["w_down"][0]))
    shared.update(_consts())
    key = (tuple(taps), stop)
    if key not in _CACHE:
        _CACHE[key] = build_program(taps, stop)
    nc = _CACHE[key]
    in_maps = []
    for i in range(NCORES):
        b0 = i * NSEQ
        m = dict(shared)
        m["x"] = f(x[b0:b0 + NSEQ].reshape(NSEQ * SEQ, D))
        m["cT"] = f(c[b0:b0 + NSEQ].T.reshape(8, 128, NSEQ).transpose(1, 0, 2))
        m["pos"] = f(positions[b0:b0 + NSEQ].astype(np.int32))
        in_maps.append(m)
    res = run_bass_kernel_spmd(nc, in_maps, core_ids=list(range(NCORES)))
    return res


def kernel(**inputs):
    res = _run(inputs)
    outs = [np.asarray(r["out"]).reshape(NSEQ, SEQ, D) for r in res.results]
    return np.concatenate(outs, axis=0).astype(np.float32)
```
